# Optimizing a Trainium2 kernel written in Bass

```python
import math
import jax, jax.numpy as jnp
from jax import lax
import numpy as np

D_MODEL = 1024
BATCH = 4
SEQ = 8192
DEPTH = 1

CTX_LEN = 256
GRID_W = 64
MIX_WIDTH = 2 * D_MODEL
SSD_DINNER = MIX_WIDTH // 2
SSD_HEADDIM = 64
SSD_HEADS = SSD_DINNER // SSD_HEADDIM
SSD_GROUPS = 2
SSD_STATE = 64
SSD_CHUNK = 128
SSD_CONV_CH = SSD_DINNER + 2 * SSD_GROUPS * SSD_STATE
GDN_WIDTH = MIX_WIDTH - SSD_DINNER
GDN_HEADDIM = 128
GDN_HEADS = GDN_WIDTH // GDN_HEADDIM
GDN_CHUNK = 64
GDN_CONV_CH = 3 * GDN_WIDTH
CONV_K = 5
N_DIR = 2
D_FF = 4 * D_MODEL
EPS = 1e-6
IN_SIZES = (SSD_DINNER, SSD_CONV_CH, N_DIR * SSD_HEADS, GDN_CONV_CH, GDN_WIDTH, N_DIR * GDN_HEADS, N_DIR * GDN_HEADS)
IN_DIM = sum(IN_SIZES)

kernel_name = 'hybrid_ssd_gdn_dit_block'


def rmsnorm(x, g):
    xf = x.astype(jnp.float32)
    y = xf * lax.rsqrt(jnp.mean(xf * xf, axis=-1, keepdims=True) + EPS)
    return y.astype(x.dtype) * g


def l2norm(x):
    xf = x.astype(jnp.float32)
    return (xf * lax.rsqrt(jnp.sum(xf * xf, axis=-1, keepdims=True) + EPS)).astype(x.dtype)


def modulate(h, shift, scale):
    return h * (1.0 + scale) + shift


def dwconv(x, w, b):
    y = lax.conv_general_dilated(x, w[:, None, :].astype(x.dtype), window_strides=(1,),
                                 padding=[(CONV_K // 2, CONV_K // 2)],
                                 dimension_numbers=('NWC', 'WIO', 'NWC'),
                                 feature_group_count=x.shape[-1])
    return y if b is None else y + b


def conv_rows(x, w, b):
    bsz, n, ch = x.shape
    rows = n // GRID_W
    return dwconv(x.reshape(bsz * rows, GRID_W, ch), w, b).reshape(bsz, n, ch)


def split_cols(p):
    parts, start = [], 0
    for size in IN_SIZES:
        parts.append(p[..., start:start + size])
        start += size
    return parts


def _ident(t):
    return t


def _flip(t):
    return jnp.flip(t, axis=1)


def ssd_scan(xh, dt, A, Bm, Cm, h0, return_y):
    bsz, n, H, P = xh.shape
    G, N = Bm.shape[2], Bm.shape[3]
    Q = SSD_CHUNK
    nc = n // Q
    rep = H // G
    x_c = xh.reshape(bsz, nc, Q, H, P)
    dt_c = dt.reshape(bsz, nc, Q, H)
    B_c = jnp.repeat(Bm, rep, axis=2).reshape(bsz, nc, Q, H, N)
    acum = jnp.cumsum(dt_c * A, axis=2)
    xdt = x_c * dt_c[..., None]
    states = jnp.einsum('bcqhn,bcqh,bcqhp->bchpn', B_c, jnp.exp(acum[:, :, -1:] - acum), xdt)
    chunk_decay = jnp.exp(acum[:, :, -1])

    def step(h, inp):
        s, d = inp
        h_next = h * d[:, :, None, None] + s
        return h_next, (h if return_y else None)

    h_final, h_prev = lax.scan(step, h0, (jnp.moveaxis(states, 1, 0), jnp.moveaxis(chunk_decay, 1, 0)))
    if not return_y:
        return None, h_final
    C_c = jnp.repeat(Cm, rep, axis=2).reshape(bsz, nc, Q, H, N)
    h_prev = jnp.moveaxis(h_prev, 0, 1)
    causal = jnp.tril(jnp.ones((Q, Q), bool))
    seg = acum[:, :, :, None, :] - acum[:, :, None, :, :]
    lmat = jnp.exp(jnp.where(causal[:, :, None], seg, -jnp.inf))
    scores = jnp.einsum('bcihn,bcjhn->bcijh', C_c, B_c) * lmat
    y = (jnp.einsum('bcijh,bcjhp->bcihp', scores, xdt)
         + jnp.einsum('bcihn,bchpn->bcihp', C_c, h_prev) * jnp.exp(acum)[..., None])
    return y.reshape(bsz, n, H, P), h_final


def gdn_scan(q, k, v, g, beta, S0, return_y):
    bsz, n, H, K = q.shape
    V = v.shape[-1]
    Q = GDN_CHUNK
    nc = n // Q

    def chunk(t):
        return jnp.swapaxes(t.reshape((bsz, nc, Q) + t.shape[2:]), 2, 3)

    qc, kc, vc = chunk(q), chunk(k), chunk(v)
    gcum = jnp.cumsum(chunk(g), axis=-1)
    bc = chunk(beta)[..., None]
    k_beta = kc * bc
    incl = jnp.tril(jnp.ones((Q, Q), bool))
    strict = jnp.tril(jnp.ones((Q, Q), bool), -1)
    decay = jnp.exp(jnp.where(incl, gcum[..., :, None] - gcum[..., None, :], -jnp.inf))
    kk = jnp.einsum('bchik,bchjk->bchij', k_beta, kc) * decay
    tmat = (jnp.eye(Q, dtype=jnp.float32) + jnp.where(strict, kk, 0.0)).astype(jnp.float32)
    rhs = jnp.concatenate([vc * bc, k_beta * jnp.exp(gcum)[..., None]], axis=-1).astype(jnp.float32)
    sol = lax.linalg.triangular_solve(tmat, rhs, left_side=True, lower=True, unit_diagonal=True)
    u, w = sol[..., :V], sol[..., V:]
    k_dec = kc * jnp.exp(gcum[..., -1:] - gcum)[..., None]
    chunk_decay = jnp.exp(gcum[..., -1])
    xs = [u, w, k_dec, chunk_decay]
    if return_y:
        q_dec = qc * jnp.exp(gcum)[..., None]
        qk = jnp.einsum('bchik,bchjk->bchij', qc, kc) * decay
        xs += [q_dec, qk]
    xs = tuple(jnp.moveaxis(t, 1, 0) for t in xs)

    def step(S, inp):
        u_i, w_i, kd_i, d_i = inp[:4]
        v_new = u_i - jnp.einsum('bhqk,bhkv->bhqv', w_i, S)
        S_next = S * d_i[..., None, None] + jnp.einsum('bhqk,bhqv->bhkv', kd_i, v_new)
        if not return_y:
            return S_next, None
        qd_i, qk_i = inp[4:]
        o = jnp.einsum('bhqk,bhkv->bhqv', qd_i, S) + jnp.einsum('bhij,bhjv->bhiv', qk_i, v_new)
        return S_next, o

    S_final, o = lax.scan(step, S0, xs)
    if not return_y:
        return None, S_final
    o = jnp.swapaxes(jnp.moveaxis(o, 0, 1), 2, 3).reshape(bsz, n, H, V)
    return o, S_final


def ssd_out_norm(y, z, g):
    bsz, n = y.shape[:2]
    v = (y.reshape(bsz, n, SSD_DINNER) * jax.nn.silu(z)).astype(jnp.float32)
    v = v.reshape(bsz, n, SSD_GROUPS, SSD_DINNER // SSD_GROUPS)
    v = v * lax.rsqrt(jnp.mean(v * v, axis=-1, keepdims=True) + EPS)
    return v.reshape(bsz, n, SSD_DINNER) * g


def gdn_out_norm(o, gate, g):
    bsz, n = o.shape[:2]
    return (rmsnorm(o, g) * jax.nn.silu(gate.reshape(bsz, n, GDN_HEADS, GDN_HEADDIM))).reshape(bsz, n, GDN_WIDTH)


def ssd_parts(xbc, dt):
    b_, l_ = xbc.shape[:2]
    xs, bm, cm = jnp.split(xbc, [SSD_DINNER, SSD_DINNER + SSD_GROUPS * SSD_STATE], axis=-1)
    return (xs.reshape(b_, l_, SSD_HEADS, SSD_HEADDIM),
            bm.reshape(b_, l_, SSD_GROUPS, SSD_STATE),
            cm.reshape(b_, l_, SSD_GROUPS, SSD_STATE),
            dt.reshape(b_, l_, N_DIR, SSD_HEADS).astype(jnp.float32))


def gdn_parts(qkv, a, beta):
    b_, l_ = qkv.shape[:2]
    q, k, v = jnp.split(qkv, 3, axis=-1)
    hs = (b_, l_, GDN_HEADS, GDN_HEADDIM)
    q = l2norm(q.reshape(hs)) * (GDN_HEADDIM ** -0.5)
    k = l2norm(k.reshape(hs))
    return (q, k, v.reshape(hs),
            a.reshape(b_, l_, N_DIR, GDN_HEADS).astype(jnp.float32),
            jax.nn.sigmoid(beta.reshape(b_, l_, N_DIR, GDN_HEADS).astype(jnp.float32)))


def token_mixers(p_lat, p_ctx, ssd_conv_w, ssd_conv_b, ssd_dt_bias, ssd_A_log, ssd_D, ssd_norm_g,
                 gdn_conv_w, gdn_dt_bias, gdn_A_log, gdn_norm_g, ctx_out):
    bsz = p_lat.shape[0]
    z_l, xbc_l, dt_l, qkv_l, gate_l, a_l, b_l = split_cols(p_lat)
    z_c, xbc_c, dt_c, qkv_c, gate_c, a_c, b_c = split_cols(p_ctx)
    xbc_l = jax.nn.silu(conv_rows(xbc_l, ssd_conv_w, ssd_conv_b))
    xbc_c = jax.nn.silu(dwconv(xbc_c, ssd_conv_w, ssd_conv_b))
    qkv_l = jax.nn.silu(conv_rows(qkv_l, gdn_conv_w, None))
    qkv_c = jax.nn.silu(dwconv(qkv_c, gdn_conv_w, None))
    xs_l, bm_l, cm_l, dt_l = ssd_parts(xbc_l, dt_l)
    xs_c, bm_c, cm_c, dt_c = ssd_parts(xbc_c, dt_c)
    q_l, k_l, v_l, a_l, be_l = gdn_parts(qkv_l, a_l, b_l)
    q_c, k_c, v_c, a_c, be_c = gdn_parts(qkv_c, a_c, b_c)

    ssd_l = xs_l * ssd_D[:, None]
    gdn_l = 0.0
    if ctx_out:
        ssd_c = xs_c * ssd_D[:, None]
        gdn_c = 0.0
    for d in range(N_DIR):
        o = _ident if d == 0 else _flip
        A = -jnp.exp(ssd_A_log[d].astype(jnp.float32))
        dtc = jax.nn.softplus(dt_c[:, :, d] + ssd_dt_bias[d])
        dtl = jax.nn.softplus(dt_l[:, :, d] + ssd_dt_bias[d])
        h0 = jnp.zeros((bsz, SSD_HEADS, SSD_HEADDIM, SSD_STATE), jnp.float32)
        yc, h_ctx = ssd_scan(o(xs_c), o(dtc), A, o(bm_c), o(cm_c), h0, ctx_out)
        yl, _ = ssd_scan(o(xs_l), o(dtl), A, o(bm_l), o(cm_l), h_ctx, True)
        ssd_l = ssd_l + o(yl)
        rate = jnp.exp(gdn_A_log[d].astype(jnp.float32))
        gc = -rate * jax.nn.softplus(a_c[:, :, d] + gdn_dt_bias[d])
        gl = -rate * jax.nn.softplus(a_l[:, :, d] + gdn_dt_bias[d])
        s0 = jnp.zeros((bsz, GDN_HEADS, GDN_HEADDIM, GDN_HEADDIM), jnp.float32)
        oc, s_ctx = gdn_scan(o(q_c), o(k_c), o(v_c), o(gc), o(be_c[:, :, d]), s0, ctx_out)
        ol, _ = gdn_scan(o(q_l), o(k_l), o(v_l), o(gl), o(be_l[:, :, d]), s_ctx, True)
        gdn_l = gdn_l + o(ol)
        if ctx_out:
            ssd_c = ssd_c + o(yc)
            gdn_c = gdn_c + o(oc)
    y_lat = jnp.concatenate([ssd_out_norm(ssd_l, z_l, ssd_norm_g),
                             gdn_out_norm(gdn_l, gate_l, gdn_norm_g)], axis=-1).astype(p_lat.dtype)
    if not ctx_out:
        return y_lat, None
    y_ctx = jnp.concatenate([ssd_out_norm(ssd_c, z_c, ssd_norm_g),
                             gdn_out_norm(gdn_c, gate_c, gdn_norm_g)], axis=-1).astype(p_ctx.dtype)
    return y_lat, y_ctx


def sq_relu_mlp(h, w1, w2):
    return jnp.square(jax.nn.relu(h @ w1)) @ w2


def setup_inputs(seed: int = 0) -> dict:
    key = jax.random.key(seed)
    ks = jax.random.split(key, 24)
    f32 = jnp.float32

    def nrm(k, shape, scale):
        return jax.random.normal(k, shape, f32) * scale

    def gain(k, shape):
        return 1.0 + nrm(k, shape, 0.02)

    def dt_bias(k, shape):
        dt = jnp.exp(jax.random.uniform(k, shape, f32, math.log(1e-3), math.log(1e-1)))
        return dt + jnp.log(-jnp.expm1(-dt))

    def a_log(k, shape):
        return jnp.log(jax.random.uniform(k, shape, f32, 1.0, 16.0))

    return {
        'x': nrm(ks[0], (BATCH, SEQ, D_MODEL), 1.0),
        'c': nrm(ks[1], (BATCH, D_MODEL), 1.0),
        'ctx': nrm(ks[2], (BATCH, CTX_LEN, D_MODEL), 1.0),
        'c_ctx': nrm(ks[3], (D_MODEL,), 1.0),
        'w_mod': nrm(ks[4], (DEPTH, D_MODEL, 6 * D_MODEL), 0.5 * D_MODEL ** -0.5),
        'b_mod': nrm(ks[5], (DEPTH, 6 * D_MODEL), 0.01),
        'norm1_g': gain(ks[6], (DEPTH, D_MODEL)),
        'w_in': nrm(ks[7], (DEPTH, D_MODEL, IN_DIM), D_MODEL ** -0.5),
        'ssd_conv_w': nrm(ks[8], (DEPTH, CONV_K, SSD_CONV_CH), CONV_K ** -0.5),
        'ssd_conv_b': nrm(ks[9], (DEPTH, SSD_CONV_CH), 0.01),
        'ssd_dt_bias': dt_bias(ks[10], (DEPTH, N_DIR, SSD_HEADS)),
        'ssd_A_log': a_log(ks[11], (DEPTH, N_DIR, SSD_HEADS)),
        'ssd_D': gain(ks[12], (DEPTH, SSD_HEADS)),
        'ssd_norm_g': gain(ks[13], (DEPTH, SSD_DINNER)),
        'gdn_conv_w': nrm(ks[14], (DEPTH, CONV_K, GDN_CONV_CH), CONV_K ** -0.5),
        'gdn_dt_bias': dt_bias(ks[15], (DEPTH, N_DIR, GDN_HEADS)),
        'gdn_A_log': a_log(ks[16], (DEPTH, N_DIR, GDN_HEADS)),
        'gdn_norm_g': gain(ks[17], (DEPTH, GDN_HEADDIM)),
        'w_out': nrm(ks[18], (DEPTH, MIX_WIDTH, D_MODEL), MIX_WIDTH ** -0.5),
        'norm2_g': gain(ks[19], (DEPTH, D_MODEL)),
        'w_mlp1': nrm(ks[20], (DEPTH, D_MODEL, D_FF), D_MODEL ** -0.5),
        'w_mlp2': nrm(ks[21], (DEPTH, D_FF, D_MODEL), D_FF ** -0.5),
        'final_g': gain(ks[22], (D_MODEL,)),
    }


def reference(x, c, ctx, c_ctx, w_mod, b_mod, norm1_g, w_in, ssd_conv_w, ssd_conv_b, ssd_dt_bias,
              ssd_A_log, ssd_D, ssd_norm_g, gdn_conv_w, gdn_dt_bias, gdn_A_log, gdn_norm_g, w_out,
              norm2_g, w_mlp1, w_mlp2, final_g):
    for l in range(DEPTH):
        ctx_out = l < DEPTH - 1
        mod_lat = (jax.nn.silu(c) @ w_mod[l] + b_mod[l])[:, None, :]
        mod_ctx = jax.nn.silu(c_ctx) @ w_mod[l] + b_mod[l]
        sh1, sc1, g1, sh2, sc2, g2 = jnp.split(mod_lat, 6, axis=-1)
        sh1c, sc1c, g1c, sh2c, sc2c, g2c = jnp.split(mod_ctx, 6, axis=-1)
        p_lat = modulate(rmsnorm(x, norm1_g[l]), sh1, sc1) @ w_in[l]
        p_ctx = modulate(rmsnorm(ctx, norm1_g[l]), sh1c, sc1c) @ w_in[l]
        y_lat, y_ctx = token_mixers(p_lat, p_ctx, ssd_conv_w[l], ssd_conv_b[l], ssd_dt_bias[l],
                                    ssd_A_log[l], ssd_D[l], ssd_norm_g[l], gdn_conv_w[l],
                                    gdn_dt_bias[l], gdn_A_log[l], gdn_norm_g[l], ctx_out)
        x = x + g1 * (y_lat @ w_out[l])
        x = x + g2 * sq_relu_mlp(modulate(rmsnorm(x, norm2_g[l]), sh2, sc2), w_mlp1[l], w_mlp2[l])
        if ctx_out:
            ctx = ctx + g1c * (y_ctx @ w_out[l])
            ctx = ctx + g2c * sq_relu_mlp(modulate(rmsnorm(ctx, norm2_g[l]), sh2c, sc2c), w_mlp1[l], w_mlp2[l])
    return rmsnorm(x, final_g)
```

```python
import math
import numpy as np
from contextlib import ExitStack
import concourse.bass as bass
import concourse.mybir as mybir
from concourse.bass_utils import run_bass_kernel_spmd

F32 = mybir.dt.float32
BF16 = mybir.dt.bfloat16
ALU = mybir.AluOpType
AF = mybir.ActivationFunctionType
AX = mybir.AxisListType
EPS = 1e-6
import os as _os
_STOP = _os.environ.get("KSTOP", "")
_KC = int(_os.environ.get("KC", "9"))
NEG = -30000.0


class Res:
    __slots__ = ("name", "last_w", "readers", "dsem", "dcnt")

    def __init__(self, name=""):
        self.name = name
        self.last_w = None
        self.readers = []
        self.dsem = None
        self.dcnt = 0


class Prog:
    ENGS = ("tensor", "vector", "scalar", "gpsimd", "sync")
    NDS = 48

    def __init__(self, nc, es):
        self.nc = nc
        self.ops = {e: [] for e in self.ENGS}
        self.nops = {e: 0 for e in self.ENGS}
        self.stage_start = {e: 0 for e in self.ENGS}
        self.semval = {e: 0 for e in self.ENGS}
        self.val = {e: {} for e in self.ENGS}
        self.maxw = {e: {} for e in self.ENGS}
        self.sems = {}
        for e in self.ENGS:
            self.sems[("E", e)] = es.enter_context(nc.semaphore("se_" + e))
        for i in range(self.NDS):
            self.sems[("D", i)] = es.enter_context(nc.semaphore("sd_%d" % i))
        self.dfree = list(range(self.NDS))
        self.dcount = {i: 0 for i in range(self.NDS)}
        self.dres = []

    def _raw(self, reads, writes):
        raw = []
        for r in reads:
            if r.last_w is not None:
                raw.append(r.last_w)
        for w in writes:
            if w.last_w is not None:
                raw.append(w.last_w)
            raw.extend(w.readers)
        return raw

    def _commit(self, tok, reads, writes):
        for r in reads:
            r.readers.append(tok)
            if len(r.readers) > 64:
                best = {}
                for t in r.readers:
                    k = t[:2]
                    if k not in best or best[k][2] < t[2]:
                        best[k] = t
                r.readers = list(best.values())
        for w in writes:
            w.last_w = tok
            w.readers = []

    def op(self, eng, fn, reads=(), writes=(), signal=True):
        idx = self.nops[eng]
        self.nops[eng] += 1
        self.ops[eng].append({"fn": fn, "raw": self._raw(reads, writes), "idx": idx})
        tok = ("E", eng, idx)
        self._commit(tok, reads, writes)
        return tok

    def dma(self, queue, pairs, sres, reads=(), writes=()):
        raw = self._raw(reads, writes)
        if sres.dsem is None:
            sres.dsem = self.dfree.pop(0)
            self.dres.append(sres)
        i = sres.dsem
        self.dcount[i] += 16 * len(pairs)
        tok = ("D", i, self.dcount[i])
        self.ops[queue].append({"dma": pairs, "raw": raw, "inc": ("D", i), "idx": None})
        self._commit(tok, reads, writes)
        return tok

    def barrier(self):
        for e in self.ENGS:
            raw = []
            for e2 in self.ENGS:
                if e2 != e and self.nops[e2] > self.stage_start[e2]:
                    raw.append(("E", e2, self.nops[e2] - 1))
            for i in range(self.NDS):
                if self.dcount[i] > 0:
                    raw.append(("D", i, self.dcount[i]))
            self.ops[e].append({"fn": None, "raw": raw, "idx": None})
        for r in self.dres:
            self.dfree.append(r.dsem)
            r.dsem = None
        self.dres = []

    def flush(self):
        nc = self.nc
        sems = self.sems
        needed = {e: set() for e in self.ENGS}
        for e in self.ENGS:
            for rec in self.ops[e]:
                w = {}
                for t in rec["raw"]:
                    if t[0] == "E":
                        _, e2, idx2 = t
                        if idx2 < self.stage_start[e2]:
                            continue
                        if e2 == e and e == "tensor":
                            continue
                        k = ("E", e2)
                        if idx2 > self.maxw[e].get(k, -1):
                            self.maxw[e][k] = idx2
                            w[k] = idx2
                    else:
                        _, i, v = t
                        k = ("D", i)
                        if v > self.maxw[e].get(k, 0):
                            self.maxw[e][k] = v
                            w[k] = v
                rec["w"] = w
                for k, v in w.items():
                    if k[0] == "E":
                        needed[k[1]].add(v)
        for e in self.ENGS:
            c = self.semval[e]
            for rec in self.ops[e]:
                rec["sig"] = False
                if rec.get("fn") is not None and rec["idx"] in needed[e]:
                    c += 1
                    rec["sig"] = True
                    self.val[e][rec["idx"]] = c
            self.semval[e] = c
        with nc.Block() as block:
            def run(engname):
                def body(eng):
                    for rec in self.ops[engname]:
                        for k, v in rec["w"].items():
                            if k[0] == "E":
                                eng.wait_ge(sems[k], self.val[k[1]][v])
                            else:
                                eng.wait_ge(sems[k], v)
                        if "dma" in rec:
                            for (o, i) in rec["dma"]:
                                eng.dma_start(out=o, in_=i).then_inc(sems[rec["inc"]], 16)
                        elif rec["fn"] is not None:
                            ins = rec["fn"](eng)
                            if rec["sig"]:
                                ins.then_inc(sems[("E", engname)], 1)
                return body

            block.tensor(run("tensor"))
            block.vector(run("vector"))
            block.scalar(run("scalar"))
            block.gpsimd(run("gpsimd"))
            block.sync(run("sync"))
        self.ops = {e: [] for e in self.ENGS}
        for e in self.ENGS:
            self.stage_start[e] = self.nops[e]


def build(NO):
    nc = bass.Bass("TRN2", target_bir_lowering=False)
    NT = 1 + 2 * NO
    TOK = NT * 256
    NCH = 2 * NT
    NOWN = NO * 256
    TS = 2

    def din(name, shape):
        return nc.dram_tensor(name, shape, F32, kind="ExternalInput").ap()

    xall = din("xall", [TOK, 1024])
    cvec = din("cvec", [128, 16])
    wmod = din("wmod", [1024, 6144])
    bmod = din("bmod", [1, 6144])
    gains = din("gains", [128, 6, 1024])
    w_in = din("w_in", [1024, 6464])
    convw = din("convw", [128, 34, 5])
    convb = din("convb", [128, 34])
    smallp = din("smallp", [128, 96])
    w_out = din("w_out", [2048, 1024])
    w1 = din("w1", [1024, 4096])
    w2 = din("w2", [4096, 1024])
    out = nc.dram_tensor("out", [NOWN, 1024], F32, kind="ExternalOutput").ap()

    def dscr(name, shape, dt):
        return nc.dram_tensor(name, shape, dt, kind="Internal").ap()

    XTOK = dscr("XTOK", [TOK, 1024], BF16)
    KTOK = dscr("KTOK", [TOK, 1024], BF16)
    VTOK = dscr("VTOK", [TOK, 1024], BF16)
    BTOK = dscr("BTOK", [TOK, 128], BF16)
    FMS = dscr("FMS", [NCH, 128, 18, 128], BF16)
    SM = dscr("SM", [TOK, 128], F32)
    ZG = dscr("ZG", [NOWN, 2048], BF16)
    YS = dscr("YS", [2, NOWN, 1024], F32)
    YG = dscr("YG", [2, NOWN, 1024], F32)
    W1B = dscr("W1B", [1024, 4096], BF16)
    W2B = dscr("W2B", [4096, 1024], BF16)
    r_chunk = [Res("ch%d" % c) for c in range(NCH)]
    r_zg = [Res() for _ in range(2 * NO)]
    r_ys = [[Res() for _ in range(2 * NO)] for _ in range(2)]
    r_yg = [[Res() for _ in range(2 * NO)] for _ in range(2)]
    r_w1b = Res("w1b")
    r_w2b = Res("w2b")

    es = ExitStack()
    P = Prog(nc, es)
    cur = [es]

    def sb(name, shape, dt=F32):
        return cur[0].enter_context(nc.sbuf_tensor(name, shape, dt)), Res(name)

    class Pool:
        def __init__(self, name, shape, dt, n):
            self.t = [sb("%s_%d" % (name, i), shape, dt) for i in range(n)]
            self.i = 0

        def next(self):
            t = self.t[self.i % len(self.t)]
            self.i += 1
            return t

    def V(m, *a, r=(), w=(), **kw):
        return P.op("vector", lambda e: getattr(e, m)(*a, **kw), r, w)

    def A(m, *a, r=(), w=(), **kw):
        return P.op("scalar", lambda e: getattr(e, m)(*a, **kw), r, w)

    def G(m, *a, r=(), w=(), **kw):
        return P.op("gpsimd", lambda e: getattr(e, m)(*a, **kw), r, w)

    def T(m, *a, r=(), w=(), sig=True, **kw):
        return P.op("tensor", lambda e: getattr(e, m)(*a, **kw), r, w, signal=sig)

    def act(out_, in_, func, r=(), w=(), **kw):
        return A("activation", out=out_, in_=in_, func=func, r=r, w=w, **kw)

    dq = [0]

    def DMA(pairs, sres, r=(), w=(), q=None):
        if q is None:
            q = "sync"
            dq[0] += 1
        return P.dma(q, pairs, sres, r, w)

    def bc(ap, shape, axis):
        return ap.unsqueeze(axis).to_broadcast(shape)

    def end_stage(ses):
        P.barrier()
        P.flush()
        ses.close()
        cur[0] = es

    def begin_stage():
        ses = ExitStack()
        cur[0] = ses
        return ses

    pbank = []
    for i in range(8):
        t = es.enter_context(nc.psum_tensor("pb%d" % i, [128, 512], F32))
        pbank.append((t, Res("pb%d" % i)))

    identf, r_c = sb("identf", [128, 128])
    identb, _ = sb("identb", [128, 128], BF16)
    onesf, _ = sb("onesf", [128, 128])
    negonesf, _ = sb("negonesf", [128, 128])
    onesb, _ = sb("onesb", [128, 128], BF16)
    ones1, _ = sb("ones1", [1, 128])
    junk, r_junk = sb("junk", [128, 1024], BF16)
    RC = [r_c]
    G("memset", onesf[:], 1.0, w=RC)
    G("memset", negonesf[:], -1.0, w=RC)
    G("memset", ones1[:], 1.0, w=RC)
    G("memset", identf[:], 1.0, w=RC)
    G("affine_select", out=identf[:], in_=identf[:], pattern=[[-1, 128]], compare_op=ALU.is_equal,
      fill=0.0, base=0, channel_multiplier=1, r=RC, w=RC)
    V("tensor_copy", identb[:], identf[:], r=RC, w=RC)
    V("tensor_copy", onesb[:], onesf[:], r=RC, w=RC)

    def tri(name, cmp_ge_free_minus_part, strict, blockdiag, fillv=0.0, onev=1.0, dt=F32, rep=1):
        tf, _ = sb(name + "_f", [128, 128])
        G("memset", tf[:], onev, w=RC)
        if cmp_ge_free_minus_part:
            G("affine_select", out=tf[:], in_=tf[:], pattern=[[1, 128]], compare_op=ALU.is_ge, fill=fillv,
              base=-strict, channel_multiplier=-1, r=RC, w=RC)
        else:
            G("affine_select", out=tf[:], in_=tf[:], pattern=[[-1, 128]], compare_op=ALU.is_ge, fill=fillv,
              base=-strict, channel_multiplier=1, r=RC, w=RC)
        if blockdiag:
            G("memset", tf[0:64, 64:128], fillv, r=RC, w=RC)
            G("memset", tf[64:128, 0:64], fillv, r=RC, w=RC)
        if dt == F32 and rep == 1:
            return tf
        t2, _ = sb(name, [128, rep, 128], dt)
        for a in range(rep):
            V("tensor_copy", t2[:, a, :], tf[:], r=RC, w=RC)
        return t2

    CST = dscr("CST", [128, 12, 1024], F32)
    WOB = dscr("WOB", [2048, 1024], BF16)
    r_cst = Res("cst")
    r_wob = Res("wob")
    ses = begin_stage()
    wstage = Pool("wst", [128, 8, 512], F32, 2)
    gn, r_gn = sb("gn", [128, 6, 1024])
    DMA([(gn[:], gains[:, :, :])], r_gn, w=[r_gn])
    modl, r_modl = sb("modl", [128, 6144])
    modc, r_modc = sb("modc", [128, 2048])
    cv, r_cv = sb("cv", [128, 16])
    scb, r_scb = sb("scb", [128, 16, 128])
    bmp = Pool("bmt", [1, 512], F32, 2)
    DMA([(cv[:], cvec[:, :])], r_cv, w=[r_cv])
    act(cv[:], cv[:], AF.Silu, r=[r_cv], w=[r_cv])
    V("tensor_copy", scb[:], bc(cv[:], [128, 16, 128], 2), r=[r_cv], w=[r_scb])
    wmod_v = wmod.rearrange("(k p) n -> p k n", p=128)
    for ng in range(12):
        wt, rw = wstage.next()
        DMA([(wt[:], wmod_v[:, :, ng * 512:(ng + 1) * 512])], rw, w=[rw])
        bmt, r_bmt = bmp.next()
        DMA([(bmt[:], bmod[:, ng * 512:(ng + 1) * 512])], r_bmt, w=[r_bmt])
        for which in range(2 if ng < 4 else 1):
            pb, rp = pbank[(ng + which) % 8]
            for k in range(8):
                T("matmul", pb[:], scb[:, which * 8 + k, :], wt[:, k, :], start=(k == 0), stop=False,
                  r=[rw, r_scb], w=[rp], sig=False)
            T("matmul", pb[:], ones1[0:1, :], bmt[0:1, :], start=False, stop=True,
              r=[r_bmt] + RC, w=[rp])
            if which == 0:
                V("tensor_copy", modl[:, ng * 512:(ng + 1) * 512], pb[:], r=[rp], w=[r_modl])
            else:
                V("tensor_copy", modc[:, ng * 512:(ng + 1) * 512], pb[:], r=[rp], w=[r_modc])
    ctmp = Pool("ctmp", [128, 1024], F32, 2)
    for slot, (src, gi_) in {0: (modl[:, 1024:2048], 0), 2: (modc[:, 1024:2048], 0), 4: (modl[:, 4096:5120], 1)}.items():
        ct, rct = ctmp.next()
        V("scalar_tensor_tensor", out=ct[:], in0=src, scalar=1.0, in1=gn[:, gi_, :], op0=ALU.add,
          op1=ALU.mult, r=[r_modl, r_modc, r_gn], w=[rct])
        DMA([(CST[:, slot, :], ct[:])], rct, r=[rct], w=[r_cst])
    DMA([(CST[:, 1, :], modl[:, 0:1024]), (CST[:, 5, :], modl[:, 3072:4096]), (CST[:, 6, :], modl[:, 2048:3072]),
         (CST[:, 7, :], modl[:, 5120:6144])], r_modl, r=[r_modl], w=[r_cst])
    DMA([(CST[:, 3, :], modc[:, 0:1024])], r_modc, r=[r_modc], w=[r_cst])
    DMA([(CST[:, 8, :], gn[:, 2, :]), (CST[:, 9, :], gn[:, 3, :]), (CST[:, 10, :], gn[:, 4, :]),
         (CST[:, 11, :], gn[:, 5, :])], r_gn, r=[r_gn], w=[r_cst])
    wcast = Pool("wcast", [128, 8, 512], BF16, 2)
    w1_v = w1.rearrange("(k p) n -> p k n", p=128)
    W1B_v = W1B.rearrange("(k p) n -> p k n", p=128)
    for pi in range(8):
        wt, rw = wstage.next()
        DMA([(wt[:], w1_v[:, :, pi * 512:(pi + 1) * 512])], rw, w=[rw])
        wc, rc_ = wcast.next()
        G("tensor_copy", wc[:], wt[:], r=[rw], w=[rc_])
        DMA([(W1B_v[:, :, pi * 512:(pi + 1) * 512], wc[:])], rc_, r=[rc_], w=[r_w1b])
    for (srcw, dstw, rdst, nfc) in ((w2, W2B, r_w2b, 32), (w_out, WOB, r_wob, 16)):
        s_v = srcw.rearrange("(f p) n -> p f n", p=128)
        d_v = dstw.rearrange("(f p) n -> p f n", p=128)
        for pi in range(nfc // 4):
            wt, rw = wstage.next()
            wtv = wt[:].rearrange("p k n -> p (k n)").rearrange("p (f n) -> p f n", f=4)
            DMA([(wtv, s_v[:, pi * 4:(pi + 1) * 4, :])], rw, w=[rw])
            wc, rc_ = wcast.next()
            wcv = wc[:].rearrange("p k n -> p (k n)").rearrange("p (f n) -> p f n", f=4)
            A("activation", out=wcv, in_=wtv, func=AF.Copy, r=[rw], w=[rc_])
            DMA([(d_v[:, pi * 4:(pi + 1) * 4, :], wcv)], rc_, r=[rc_], w=[rdst])
    end_stage(ses)
    if _STOP == '0':
        es.close()
        return nc

    ses = begin_stage()
    WIN, r_win = sb("WIN", [128, 8, 6464], BF16)
    wstage = Pool("wstA", [128, 8, 512], F32, 1)
    win_v = w_in.rearrange("(k p) n -> p k n", p=128)
    for pi in range(13):
        c0 = pi * 512
        c1 = min(6464, c0 + 512)
        wt, rw = wstage.next()
        DMA([(wt[:, :, 0:c1 - c0], win_v[:, :, c0:c1])], rw, w=[rw])
        if pi % 2 == 0:
            act(WIN[:, :, c0:c1], wt[:, :, 0:c1 - c0], AF.Copy, r=[rw], w=[r_win])
        else:
            V("tensor_copy", WIN[:, :, c0:c1], wt[:, :, 0:c1 - c0], r=[rw], w=[r_win])
    cstA, r_cstA = sb("cstA", [128, 4, 1024])
    DMA([(cstA[:], CST[:, 0:4, :])], r_cstA, r=[r_cst], w=[r_cstA])
    A1, B1, A1c, B1c = cstA[:, 0, :], cstA[:, 1, :], cstA[:, 2, :], cstA[:, 3, :]
    cw, r_cw = sb("cw", [128, 34, 5])
    cbt, r_cb = sb("cbt", [128, 34])
    DMA([(cw[:], convw[:, :, :])], r_cw, w=[r_cw])
    DMA([(cbt[:], convb[:, :])], r_cb, w=[r_cb])
    spp, r_spp = sb("spp", [128, 96])
    coef, r_coef = sb("coef", [128, 48])
    DMA([(spp[:], smallp[:, :])], r_spp, w=[r_spp])
    act(coef[:], spp[:, 48:96], AF.Exp, r=[r_spp], w=[r_coef])
    V("tensor_scalar_mul", coef[:], coef[:], -1.0, r=[r_coef], w=[r_coef])

    xpool = Pool("xt", [128, 2, 1024], F32, 1)
    statp = Pool("stat", [128, 8], F32, 3)
    hfp = Pool("hf", [128, 1024], F32, 1)
    hbp = Pool("hb", [128, 2, 1024], BF16, 1)
    hTp = Pool("hT", [128, 8, 256], BF16, 1)
    caccp = Pool("cacc", [128, 256], F32, 3)
    fmbp = Pool("fmb", [128, 34, 256], BF16, 1)
    qkfp = Pool("qkf", [128, 16, 256], BF16, 1)
    sqbp = Pool("sqb", [128, 512], BF16, 2)
    lnqp = Pool("lnq", [128, 512], F32, 1)
    rnqp = Pool("rnq", [128, 512], F32, 1)
    tokp = Pool("tok", [128, 1024], BF16, 2)
    btkp = Pool("btkA", [128, 128], BF16, 2)
    smop = Pool("smo", [128, 96], F32, 2)
    sptp = Pool("spt", [128, 64], F32, 2)
    zgp = Pool("zgt", [128, 2048], BF16, 1)
    FMCOL = list(range(34))
    LNQS = math.log(128.0 ** -0.5)

    for ti in range(NT):
        kind = "ctx" if ti == 0 else ("own" if ti <= NO else "far")
        RL = 256 if kind == "ctx" else 64
        nfm = 34 if kind == "own" else 25
        Abc, Bbc = (A1c, B1c) if kind == "ctx" else (A1, B1)
        rAB = [r_cstA]
        tok0 = ti * 256
        xt, rx = xpool.next()
        DMA([(xt[:], xall[tok0:tok0 + 256, :].rearrange("(s p) d -> p s d", p=128))], rx, w=[rx])
        st, rst = statp.next()
        G("memset", st[:], 0.0, w=[rst])
        for s in range(2):
            act(junk[:], xt[:, s, :], AF.Square, accum_out=st[:, s:s + 1], r=[rx, rst], w=[r_junk, rst])
        act(st[:, 2:4], st[:, 0:2], AF.Ln, scale=1.0 / 1024, bias=EPS, r=[rst], w=[rst])
        act(st[:, 4:6], st[:, 2:4], AF.Exp, scale=-0.5, r=[rst], w=[rst])
        hb, rhb = hbp.next()
        for s in range(2):
            hf, rhf = hfp.next()
            V("scalar_tensor_tensor", out=hf[:], in0=xt[:, s, :], scalar=st[:, 4 + s:5 + s], in1=Abc,
              op0=ALU.mult, op1=ALU.mult, r=[rx, rst] + rAB, w=[rhf])
            V("tensor_tensor", hb[:, s, :], hf[:], Bbc, ALU.add, r=[rhf] + rAB, w=[rhb])
        hT, rhT = hTp.next()
        for half in range(2):
            pb, rp = pbank[half]
            pv = pb[:].bitcast(BF16).rearrange("p (k t) -> p k t", k=4)
            for kk in range(4):
                k = half * 4 + kk
                for s in range(2):
                    T("transpose", pv[:, kk, s * 128:(s + 1) * 128], hb[:, s, k * 128:(k + 1) * 128], identb[:],
                      r=[rhb] + RC, w=[rp], sig=(kk == 3 and s == 1))
            if half == 0:
                act(hT[:, 0:4, :], pv, AF.Copy, r=[rp], w=[rhT])
            else:
                V("tensor_copy", hT[:, 4:8, :], pv, r=[rp], w=[rhT])
        fmb, rfm = fmbp.next()
        qkf, rqk = qkfp.next()
        for ci in range(nfm):
            pb, rp = pbank[2 + (ci % 2)]
            half = (ci // 2) % 2
            pc = pb[:, half * 256:(half + 1) * 256]
            for k in range(8):
                T("matmul", pc, WIN[:, k, ci * 128:(ci + 1) * 128], hT[:, k, :], start=(k == 0), stop=(k == 7),
                  r=[r_win, rhT], w=[rp], sig=(k == 7))
            ca, rca = caccp.next()
            pv = pc.rearrange("p (r l) -> p r l", l=RL)
            av = ca[:].rearrange("p (r l) -> p r l", l=RL)
            V("tensor_scalar", av, pv, cw[:, ci, 2:3], cbt[:, ci:ci + 1], ALU.mult, ALU.add,
              r=[rp, r_cw, r_cb], w=[rca])
            for j in (0, 1, 3, 4):
                sft = j - 2
                if sft > 0:
                    o_sl, i_sl = slice(0, RL - sft), slice(sft, RL)
                else:
                    o_sl, i_sl = slice(-sft, RL), slice(0, RL + sft)
                V("scalar_tensor_tensor", out=av[:, :, o_sl], in0=pv[:, :, i_sl], scalar=cw[:, ci, j:j + 1],
                  in1=av[:, :, o_sl], op0=ALU.mult, op1=ALU.add, r=[rp, r_cw, rca], w=[rca])
            isk = 9 <= ci <= 16
            isq = 26 <= ci <= 33
            if isk or isq:
                qi = (ci - 9) if isk else (8 + ci - 26)
                act(qkf[:, qi, :], ca[:], AF.Silu, r=[rca], w=[rqk])
            else:
                act(fmb[:, ci, :], ca[:], AF.Silu, r=[rca], w=[rfm])
        if kind == "own":
            for s in range(2):
                zg, rzg = zgp.next()
                for cg in range(4):
                    pb, rp = pbank[7]
                    for k in range(8):
                        T("matmul", pb[:], hT[:, k, s * 128:(s + 1) * 128],
                          WIN[:, k, 4416 + cg * 512:4416 + (cg + 1) * 512], start=(k == 0), stop=(k == 7),
                          r=[r_win, rhT], w=[rp], sig=(k == 7))
                    act(zg[:, cg * 512:(cg + 1) * 512], pb[:], AF.Silu, r=[rp], w=[rzg])
                row = (ti - 1) * 256 + s * 128
                DMA([(ZG[row:row + 128, :], zg[:])], rzg, r=[rzg], w=[r_zg[(ti - 1) * 2 + s]])
        ngrp = 8 if kind == "own" else 4
        for pr in range(ngrp):
            pb, rp = pbank[4]
            sqb, rsq = sqbp.next()
            for u in range(2):
                qi = pr * 2 + u
                act(sqb[:, u * 256:(u + 1) * 256], qkf[:, qi, :], AF.Square, r=[rqk], w=[rsq])
            T("matmul", pb[:], onesb[:], sqb[:], start=True, stop=True, r=[rsq] + RC, w=[rp])
            lnq, rln = lnqp.next()
            rnq, rrn = rnqp.next()
            act(lnq[:], pb[:], AF.Ln, bias=EPS, r=[rp], w=[rln])
            act(rnq[:], lnq[:], AF.Exp, scale=-0.5, bias=(LNQS if pr >= 4 else 0.0), r=[rln], w=[rrn])
            for u in range(2):
                qi = pr * 2 + u
                ci = (9 + qi) if qi < 8 else (26 + qi - 8)
                V("tensor_tensor", fmb[:, ci, :], qkf[:, qi, :], rnq[:, u * 256:(u + 1) * 256], ALU.mult,
                  r=[rqk, rrn], w=[rfm])
        for s in range(2):
            pb, rp = pbank[7]
            for k in range(8):
                T("matmul", pb[:, 0:64], hT[:, k, s * 128:(s + 1) * 128], WIN[:, k, 4352:4416], start=(k == 0),
                  stop=(k == 7), r=[r_win, rhT], w=[rp], sig=(k == 7))
            spt, rsp = sptp.next()
            smo, rsm = smop.next()
            V("tensor_tensor", spt[:, 0:48], pb[:, 0:48], spp[:, 0:48], ALU.add, r=[rp, r_spp], w=[rsp])
            act(spt[:, 48:64], pb[:, 48:64], AF.Exp, scale=-1.0, r=[rp], w=[rsp])
            act(spt[:, 0:48], spt[:, 0:48], AF.Exp, r=[rsp], w=[rsp])
            act(spt[:, 0:48], spt[:, 0:48], AF.Ln, bias=1.0, r=[rsp], w=[rsp])
            V("tensor_copy", smo[:, 0:32], spt[:, 0:32], r=[rsp], w=[rsm])
            V("tensor_tensor", smo[:, 32:80], spt[:, 0:48], coef[:], ALU.mult, r=[rsp, r_coef], w=[rsm])
            V("tensor_scalar_add", spt[:, 48:64], spt[:, 48:64], 1.0, r=[rsp], w=[rsp])
            V("reciprocal", smo[:, 80:96], spt[:, 48:64], r=[rsp], w=[rsm])
            c = 2 * ti + s
            DMA([(SM[c * 128:(c + 1) * 128, 0:96], smo[:])], rsm, r=[rsm], w=[r_chunk[c]])
        for s in range(2):
            c = 2 * ti + s
            ts_ = slice(s * 128, (s + 1) * 128)
            pairs = [(FMS[c, :, 0:1, :], fmb[:, 8:9, ts_]), (FMS[c, :, 2:10, :], fmb[:, 9:17, ts_])]
            if kind == "own":
                pairs += [(FMS[c, :, 1:2, :], fmb[:, 25:26, ts_]), (FMS[c, :, 10:18, :], fmb[:, 26:34, ts_])]
            DMA(pairs, rfm, r=[rfm], w=[r_chunk[c]])
            for gi, (c0, dst) in enumerate(((0, XTOK), (9, KTOK), (17, VTOK))):
                pb, rp = pbank[5 + (gi % 2)]
                pv = pb[:].bitcast(BF16)
                for j in range(8):
                    T("transpose", pv[:, j * 128:(j + 1) * 128], fmb[:, c0 + j, ts_], identb[:], r=[rfm] + RC,
                      w=[rp], sig=(j == 7))
                tk, rtk = tokp.next()
                if gi % 2 == 0:
                    act(tk[:], pv, AF.Copy, r=[rp], w=[rtk])
                else:
                    V("tensor_copy", tk[:], pv, r=[rp], w=[rtk])
                DMA([(dst[c * 128:(c + 1) * 128, :], tk[:])], rtk, r=[rtk], w=[r_chunk[c]])
            pb, rp = pbank[6]
            pv = pb[:].bitcast(BF16)
            T("transpose", pv[:, 0:128], fmb[:, 8, ts_], identb[:], r=[rfm] + RC, w=[rp])
            bt_, rbt = btkp.next()
            V("tensor_copy", bt_[:], pv[:, 0:128], r=[rp], w=[rbt])
            DMA([(BTOK[c * 128:(c + 1) * 128, :], bt_[:])], rbt, r=[rbt], w=[r_chunk[c]])

    end_stage(ses)
    if _STOP == 'A':
        es.close()
        return nc
    ses = begin_stage()
    Mincl = [tri("mi0", True, 0, False), tri("mi1", False, 0, False)]
    Maft = [tri("ma0", False, 1, False), tri("ma1", True, 1, False)]
    NegT = [tri("ng0", True, 0, False, fillv=NEG, onev=0.0, dt=BF16, rep=4),
            tri("ng1", False, 0, False, fillv=NEG, onev=0.0, dt=BF16, rep=4)]
    statp = Pool("statB", [128, 8], F32, 2)
    seqP = [0, 1] + list(range(2, 2 + 2 * NO))
    seqQ = [1, 0] + list(range(NCH - 1, 2 + 2 * NO - 1, -1)) + list(range(2 + 2 * NO - 1, 1, -1))

    def is_own(c):
        return 2 <= c < 2 + 2 * NO

    xtkp = Pool("xtkB", [128, 1024], BF16, 3)
    btkB = Pool("btkB", [128, 128], BF16, 3)
    fm2p = Pool("fm2", [128, 2, 128], BF16, 3)
    smp = Pool("smB", [128, 96], F32, 3)
    exp_ = Pool("exB", [128, 64], F32, 3)
    xdtp = Pool("xdt", [128, 16, 64], BF16, 2)
    xdwp = Pool("xdw", [128, 16, 64], BF16, 2)
    g2p = Pool("g2", [128, 16, 128], F32, 1)
    argp = Pool("arg", [128, 8, 128], F32, 2)
    ltp = Pool("lt", [128, 8, 128], BF16, 2)
    stp = Pool("stt", [128, 8, 128], BF16, 2)
    tmpp = Pool("tmpB", [128, 8, 64], F32, 2)
    yop = Pool("yo", [128, 1024], F32, 2)
    sst = [sb("sst%d" % d, [128, 512]) for d in range(2)]
    sstb = [sb("sstb%d" % d, [128, 512], BF16) for d in range(2)]
    for d in range(2):
        G("memset", sst[d][0][:], 0.0, w=[sst[d][1]])
        G("memset", sstb[d][0][:], 0.0, w=[sstb[d][1]])

    def ssd_chunk(d, c):
        rev = d
        outp = is_own(c)
        xtk, rxt = xtkp.next()
        btk, rbt = btkB.next()
        fm2, rf2 = fm2p.next()
        sm, rsm = smp.next()
        rc_ = r_chunk[c]
        DMA([(xtk[:], XTOK[c * 128:(c + 1) * 128, :])], rxt, r=[rc_], w=[rxt])
        DMA([(btk[:], BTOK[c * 128:(c + 1) * 128, :])], rbt, r=[rc_], w=[rbt])
        if outp:
            DMA([(fm2[:], FMS[c, :, 0:2, :])], rf2, r=[rc_], w=[rf2])
        DMA([(sm[:], SM[c * 128:(c + 1) * 128, 0:96])], rsm, r=[rc_], w=[rsm])
        dt = sm[:, d * 16:(d + 1) * 16]
        dA = sm[:, 32 + d * 16:32 + (d + 1) * 16]
        st_, rst = sst[d]
        stb, rstb = sstb[d]
        pS, rpS = pbank[0]
        T("matmul", pS[:, 0:16], Mincl[rev][:], dA, start=True, stop=True, r=[rsm] + RC, w=[rpS], sig=False)
        T("matmul", pS[:, 16:32], Maft[rev][:], dA, start=True, stop=True, r=[rsm] + RC, w=[rpS], sig=False)
        T("matmul", pS[:, 32:48], onesf[:], dA, start=True, stop=True, r=[rsm] + RC, w=[rpS])
        ex, rex = exp_.next()
        act(ex[:, 0:48], pS[:, 0:48], AF.Exp, r=[rpS], w=[rex])
        V("tensor_copy", ex[:, 48:64], pS[:, 0:16], r=[rpS], w=[rex])
        ex2, rex2 = exp_.next()
        V("tensor_tensor", ex2[:, 0:16], dt, ex[:, 16:32], ALU.mult, r=[rsm, rex], w=[rex2])
        xv = xtk[:].rearrange("p (h q) -> p h q", h=16)
        xdw, rxw = xdwp.next()
        V("tensor_tensor", xdw[:], xv, bc(ex2[:, 0:16], [128, 16, 64], 2), ALU.mult, r=[rxt, rex2], w=[rxw])
        if outp:
            xdt, rxd = xdtp.next()
            V("tensor_tensor", xdt[:], xv, bc(dt, [128, 16, 64], 2), ALU.mult, r=[rxt, rsm], w=[rxd])
            g2, rg2 = g2p.next()
            V("tensor_tensor", g2[:], bc(Mincl[rev][:], [128, 16, 128], 1), bc(dA, [128, 16, 128], 2), ALU.mult,
              r=[rsm] + RC, w=[rg2])
            yo, ryo = yop.next()
            for hh in range(2):
                for a in range(2):
                    pL, rpL = pbank[1 + a]
                    T("matmul", pL[:], onesf[:], g2[:, hh * 8 + a * 4:hh * 8 + a * 4 + 4, :].rearrange("p h i -> p (h i)"), start=True, stop=False,
                      r=[rg2] + RC, w=[rpL], sig=False)
                    T("matmul", pL[:], identb[:], NegT[rev][:].rearrange("p h i -> p (h i)"), start=False, stop=True, r=RC, w=[rpL])
                arg, rar = argp.next()
                for a in range(2):
                    pL, rpL = pbank[1 + a]
                    h0 = hh * 8 + a * 4
                    V("tensor_tensor", arg[:, a * 4:(a + 1) * 4, :], pL[:].rearrange("p (h i) -> p h i", h=4),
                      bc(ex[:, 48 + h0:48 + h0 + 4], [128, 4, 128], 2), ALU.subtract, r=[rpL, rex], w=[rar])
                lt, rlt = ltp.next()
                act(lt[:], arg[:], AF.Exp, r=[rar], w=[rlt])
                gR = slice(hh * 64, (hh + 1) * 64)
                pC, rpC = pbank[3]
                T("matmul", pC[:, 0:128], fm2[gR, 0, :], fm2[gR, 1, :], start=True, stop=True, r=[rf2], w=[rpC])
                stt, rstt = stp.next()
                V("tensor_tensor", stt[:], lt[:], bc(pC[:, 0:128], [128, 8, 128], 1), ALU.mult, r=[rlt, rpC], w=[rstt])
                pI, rpI = pbank[4]
                T("matmul", pI[:], fm2[gR, 1, :], stb[gR, :], start=True, stop=True, r=[rf2, rstb], w=[rpI])
                pA, rpA = pbank[5]
                for h8 in range(8):
                    T("matmul", pA[:, h8 * 64:(h8 + 1) * 64], stt[:, h8, :], xdt[:, hh * 8 + h8, :], start=True,
                      stop=True, r=[rstt, rxd], w=[rpA], sig=(h8 == 7))
                tmp, rtm = tmpp.next()
                V("tensor_tensor", tmp[:], pI[:].rearrange("p (h q) -> p h q", h=8),
                  bc(ex[:, hh * 8:hh * 8 + 8], [128, 8, 64], 2), ALU.mult, r=[rpI, rex], w=[rtm])
                V("tensor_tensor", yo[:, hh * 512:(hh + 1) * 512], pA[:], tmp[:].rearrange("p h q -> p (h q)"), ALU.add,
                  r=[rpA, rtm], w=[ryo])
            row = (c - 2) * 128
            DMA([(YS[d, row:row + 128, :], yo[:])], ryo, r=[ryo], w=[r_ys[d][c - 2]])
        for g in range(2):
            gR = slice(g * 64, (g + 1) * 64)
            pU, rpU = pbank[6 + g]
            T("matmul", pU[:], btk[:], xdw[:, g * 8:(g + 1) * 8, :].rearrange("p h q -> p (h q)"), start=True,
              stop=True, r=[rbt, rxw], w=[rpU])
            sv = st_[gR, :].rearrange("p (h q) -> p h q", h=8)
            V("tensor_tensor", sv, sv, bc(ex[gR, 32 + g * 8:32 + g * 8 + 8], [64, 8, 64], 2), ALU.mult,
              r=[rst, rex], w=[rst])
            V("tensor_tensor", st_[gR, :], st_[gR, :], pU[gR, :], ALU.add, r=[rst, rpU], w=[rst])
            act(stb[gR, :], st_[gR, :], AF.Copy, r=[rst], w=[rstb])

    for i in range(len(seqQ)):
        ssd_chunk(1, seqQ[i])
        if i < len(seqP):
            ssd_chunk(0, seqP[i])

    end_stage(ses)
    if _STOP == 'B':
        es.close()
        return nc
    ses = begin_stage()
    MinclB = [tri("mib0", True, 0, True), tri("mib1", False, 0, True)]
    MaftB = [tri("mab0", False, 1, True), tri("mab1", True, 1, True)]
    NegP = [tri("np0", False, 0, True, fillv=NEG, onev=0.0, dt=BF16, rep=4),
            tri("np1", True, 0, True, fillv=NEG, onev=0.0, dt=BF16, rep=4)]
    StrictP = [tri("sp0", False, 1, True, dt=BF16, rep=4), tri("sp1", True, 1, True, dt=BF16, rep=4)]
    NegTB = [tri("ntb0", True, 0, True, fillv=NEG, onev=0.0, dt=BF16, rep=4),
             tri("ntb1", False, 0, True, fillv=NEG, onev=0.0, dt=BF16, rep=4)]
    StrictTBn = [tri("stb0", True, 1, True, onev=-1.0, dt=BF16, rep=4), tri("stb1", False, 1, True, onev=-1.0, dt=BF16, rep=4)]
    identrep, _ = sb("identrep", [128, 4, 128])
    for a in range(4):
        V("tensor_copy", identrep[:, a, :], identf[:], r=RC, w=RC)
    onesA, _ = sb("onesA", [128, 128])
    onesB, _ = sb("onesB", [128, 128])
    G("memset", onesA[:], 0.0, w=RC)
    G("memset", onesB[:], 0.0, w=RC)
    G("memset", onesA[0:64, :], 1.0, r=RC, w=RC)
    G("memset", onesB[64:128, :], 1.0, r=RC, w=RC)
    fmkp = Pool("fmk", [128, 16, 128], BF16, 2)
    ktkp = Pool("ktk", [128, 1024], BF16, 2)
    vtkp = Pool("vtk", [128, 1024], BF16, 2)
    smC = Pool("smC", [128, 96], F32, 2)
    exC = Pool("exC", [128, 64], F32, 2)
    g2c = Pool("g2c", [128, 4, 128], F32, 2)
    argc = Pool("argc", [128, 4, 128], F32, 2)
    dpp = Pool("dp", [128, 4, 128], BF16, 2)
    dpsp = Pool("dps", [128, 4, 128], BF16, 2)
    qkp = Pool("qkm", [128, 4, 128], BF16, 2)
    dptp = Pool("dpt", [128, 4, 128], BF16, 2)
    wtp = Pool("wtm", [128, 4, 128], F32, 2)
    qktp = Pool("qkt", [128, 4, 128], BF16, 2)
    PP = Pool("Pm", [128, 4, 128], F32, 3)
    PTP = Pool("PTm", [128, 4, 128], F32, 3)
    XP = Pool("Xm", [128, 4, 128], F32, 3)
    xbp = Pool("xb", [128, 4, 128], BF16, 2)
    vbp = Pool("vb", [128, 4, 128], BF16, 2)
    kbgp = Pool("kbg", [128, 4, 128], BF16, 2)
    kdcp = Pool("kdc", [128, 4, 128], BF16, 2)
    egrp = Pool("egr", [128, 4, 128], BF16, 2)
    qdtp = Pool("qdt", [128, 4, 128], BF16, 2)
    nwtp = Pool("nwt", [128, 4, 128], BF16, 2)
    vnbp = Pool("vnb", [128, 4, 128], BF16, 2)
    goutp = Pool("gout", [128, 1024], F32, 2)
    gS = [sb("gS%d" % d, [128, 8, 128]) for d in range(2)]
    gSb = [sb("gSb%d" % d, [128, 8, 128], BF16) for d in range(2)]
    for d in range(2):
        G("memset", gS[d][0][:], 0.0, w=[gS[d][1]])
        G("memset", gSb[d][0][:], 0.0, w=[gSb[d][1]])

    def gdn_chunk(d, c):
        rev = d
        outp = is_own(c)
        rc_ = r_chunk[c]
        fmk, rfk = fmkp.next()
        ktk, rkt = ktkp.next()
        vtk, rvt = vtkp.next()
        sm, rsm = smC.next()
        DMA([(fmk[:, 0:(16 if outp else 8), :], FMS[c, :, 2:(18 if outp else 10), :])], rfk, r=[rc_], w=[rfk])
        DMA([(ktk[:], KTOK[c * 128:(c + 1) * 128, :])], rkt, r=[rc_], w=[rkt])
        DMA([(vtk[:], VTOK[c * 128:(c + 1) * 128, :])], rvt, r=[rc_], w=[rvt])
        DMA([(sm[:], SM[c * 128:(c + 1) * 128, 0:96])], rsm, r=[rc_], w=[rsm])
        g = sm[:, 64 + d * 8:64 + (d + 1) * 8]
        beta = sm[:, 80 + d * 8:80 + (d + 1) * 8]
        S_, rS = gS[d]
        Sb, rSb = gSb[d]
        pS, rpS = pbank[7]
        T("matmul", pS[:, 0:8], MinclB[rev][:], g, start=True, stop=True, r=[rsm] + RC, w=[rpS], sig=False)
        T("matmul", pS[:, 8:16], MaftB[rev][:], g, start=True, stop=True, r=[rsm] + RC, w=[rpS], sig=False)
        T("matmul", pS[:, 16:24], onesA[:], g, start=True, stop=True, r=[rsm] + RC, w=[rpS], sig=False)
        T("matmul", pS[:, 24:32], onesB[:], g, start=True, stop=True, r=[rsm] + RC, w=[rpS])
        ex, rex = exC.next()
        act(ex[:, 0:32], pS[:, 0:32], AF.Exp, r=[rpS], w=[rex])
        V("tensor_copy", ex[:, 32:40], pS[:, 0:8], r=[rpS], w=[rex])
        V("tensor_scalar_mul", ex[:, 40:48], beta, -1.0, r=[rsm], w=[rex])
        V("tensor_tensor", ex[:, 48:56], beta, ex[:, 0:8], ALU.mult, r=[rsm, rex], w=[rex])
        gout, rgo = goutp.next() if outp else (None, None)
        for hg in range(2):
            h0 = hg * 4
            hs = slice(h0, h0 + 4)
            g2, rg2 = g2c.next()
            V("tensor_tensor", g2[:], bc(MinclB[rev][:], [128, 4, 128], 1), bc(g[:, hs], [128, 4, 128], 2), ALU.mult,
              r=[rsm] + RC, w=[rg2])
            pZ, rpZ = pbank[0]
            g2f = g2[:].rearrange("p h i -> p (h i)")
            T("matmul", pZ[:], negonesf[:], g2f, start=True, stop=False, r=[rg2] + RC, w=[rpZ], sig=False)
            T("matmul", pZ[:], identb[:], NegP[rev][:].rearrange("p h i -> p (h i)"), start=False, stop=True, r=RC,
              w=[rpZ])
            arg, rar = argc.next()
            V("tensor_tensor", arg[:], pZ[:].rearrange("p (h i) -> p h i", h=4), bc(ex[:, 32 + h0:32 + h0 + 4], [128, 4, 128], 2),
              ALU.add, r=[rpZ, rex], w=[rar])
            dp, rdp = dpp.next()
            act(dp[:], arg[:], AF.Exp, r=[rar], w=[rdp])
            dps, rds = dpsp.next()
            V("tensor_tensor", dps[:], dp[:], StrictP[rev][:], ALU.mult, r=[rdp] + RC, w=[rds])
            pK, rpK = pbank[1]
            for a in range(4):
                T("matmul", pK[:, a * 128:(a + 1) * 128], fmk[:, h0 + a, :], fmk[:, h0 + a, :], start=True, stop=True,
                  r=[rfk], w=[rpK], sig=(a == 3))
            Am, rA = PP.next()
            for a in range(4):
                V("scalar_tensor_tensor", out=Am[:, a, :], in0=pK[:, a * 128:(a + 1) * 128],
                  scalar=ex[:, 40 + h0 + a:41 + h0 + a], in1=dps[:, a, :], op0=ALU.mult, op1=ALU.mult,
                  r=[rpK, rex, rds], w=[rA])
            pZ2, rpZ2 = pbank[3]
            T("matmul", pZ2[:], onesf[:], g2f, start=True, stop=False, r=[rg2] + RC, w=[rpZ2], sig=False)
            T("matmul", pZ2[:], identb[:], NegTB[rev][:].rearrange("p h i -> p (h i)"), start=False, stop=True, r=RC,
              w=[rpZ2])
            arg2, rar2 = argc.next()
            V("tensor_tensor", arg2[:], pZ2[:].rearrange("p (h i) -> p h i", h=4), bc(ex[:, 32 + h0:32 + h0 + 4], [128, 4, 128], 2),
              ALU.subtract, r=[rpZ2, rex], w=[rar2])
            dpt, rdpt = dptp.next()
            act(dpt[:], arg2[:], AF.Exp, r=[rar2], w=[rdpt])
            bd, rbd = g2c.next()
            V("tensor_tensor", bd[:], bc(identf[:], [128, 4, 128], 1), bc(beta[:, hs], [128, 4, 128], 2), ALU.mult,
              r=[rsm] + RC, w=[rbd])
            pBr, rpBr = pbank[3]
            T("matmul", pBr[:], onesf[:], bd[:].rearrange("p h i -> p (h i)"), start=True, stop=True, r=[rbd] + RC, w=[rpBr])
            wt_, rwt = wtp.next()
            V("tensor_tensor", wt_[:], pBr[:].rearrange("p (h i) -> p h i", h=4), StrictTBn[rev][:], ALU.mult, r=[rpBr] + RC, w=[rwt])
            V("tensor_tensor", wt_[:], wt_[:], dpt[:], ALU.mult, r=[rwt, rdpt], w=[rwt])
            ATm, rAT = PTP.next()
            Xm, rX = XP.next()
            V("tensor_tensor", ATm[:], pK[:].rearrange("p (h i) -> p h i", h=4), wt_[:], ALU.mult, r=[rpK, rwt], w=[rAT])
            V("tensor_tensor", Xm[:], ATm[:], identrep[:], ALU.add, r=[rAT] + RC, w=[rX])
            if outp:
                pQ, rpQ = pbank[2]
                for a in range(4):
                    T("matmul", pQ[:, a * 128:(a + 1) * 128], fmk[:, h0 + a, :], fmk[:, 8 + h0 + a, :], start=True,
                      stop=True, r=[rfk], w=[rpQ], sig=(a == 3))
                qkt, rqt = qktp.next()
                V("tensor_tensor", qkt[:], pQ[:].rearrange("p (h i) -> p h i", h=4), dpt[:], ALU.mult, r=[rpQ, rdpt], w=[rqt])
                pG, rpG = pbank[0]
                T("matmul", pG[:], onesf[:], g2f, start=True, stop=True, r=[rg2] + RC, w=[rpG])
                egr, reg = egrp.next()
                act(egr[:], pG[:].rearrange("p (h i) -> p h i", h=4), AF.Exp, r=[rpG], w=[reg])
                qdt, rqd = qdtp.next()
                V("tensor_tensor", qdt[:], fmk[:, 8 + h0:8 + h0 + 4, :], egr[:], ALU.mult, r=[rfk, reg], w=[rqd])
            if _KC <= 2:
                continue
            Pm, rP, PTm, rPT = Am, rA, ATm, rAT
            for s in range(6):
                last = (s == 5)
                if not last:
                    pP_, rpP = pbank[1]
                    for a in range(4):
                        T("matmul", pP_[:, a * 128:(a + 1) * 128], PTm[:, a, :], Pm[:, a, :], start=True, stop=True,
                          r=[rP, rPT], w=[rpP], sig=(a == 3))
                    if s < 4:
                        pPT_, rpPT = pbank[2]
                        for a in range(4):
                            T("matmul", pPT_[:, a * 128:(a + 1) * 128], Pm[:, a, :], PTm[:, a, :], start=True,
                              stop=True, r=[rP, rPT], w=[rpPT], sig=(a == 3))
                if s >= 1:
                    pX, rpX = pbank[0]
                    for a in range(4):
                        T("matmul", pX[:, a * 128:(a + 1) * 128], identf[:], Xm[:, a, :], start=True, stop=False,
                          r=[rX] + RC, w=[rpX], sig=False)
                        T("matmul", pX[:, a * 128:(a + 1) * 128], Pm[:, a, :], Xm[:, a, :], start=False, stop=True,
                          r=[rX, rP], w=[rpX], sig=(a == 3))
                if s >= 1:
                    if last:
                        xb, rxb = xbp.next()
                        V("tensor_copy", xb[:], pX[:].rearrange("p (h i) -> p h i", h=4), r=[rpX], w=[rxb])
                    else:
                        Xn, rXn = XP.next()
                        V("tensor_copy", Xn[:], pX[:].rearrange("p (h i) -> p h i", h=4), r=[rpX], w=[rXn])
                        Xm, rX = Xn, rXn
                if not last:
                    Pn, rPn = PP.next()
                    act(Pn[:], pP_[:].rearrange("p (h i) -> p h i", h=4), AF.Copy, r=[rpP], w=[rPn])
                    if s < 4:
                        PTn, rPTn = PTP.next()
                        act(PTn[:], pPT_[:].rearrange("p (h i) -> p h i", h=4), AF.Copy, r=[rpPT], w=[rPTn])
                        PTm, rPT = PTn, rPTn
                    Pm, rP = Pn, rPn
            if _KC <= 3:
                continue
            kv = ktk[:].rearrange("p (h q) -> p h q", h=8)[:, hs, :]
            vv = vtk[:].rearrange("p (h q) -> p h q", h=8)[:, hs, :]
            vb, rvb = vbp.next()
            kbg, rkb = kbgp.next()
            kdc, rkd = kdcp.next()
            V("tensor_tensor", vb[:], vv, bc(beta[:, hs], [128, 4, 128], 2), ALU.mult, r=[rvt, rsm], w=[rvb])
            V("tensor_tensor", kbg[:], kv, bc(ex[:, 48 + h0:48 + h0 + 4], [128, 4, 128], 2), ALU.mult, r=[rkt, rex], w=[rkb])
            V("tensor_tensor", kdc[:], kv, bc(ex[:, 8 + h0:8 + h0 + 4], [128, 4, 128], 2), ALU.mult, r=[rkt, rex], w=[rkd])
            pW, rpW = pbank[1]
            for a in range(4):
                T("matmul", pW[:, a * 128:(a + 1) * 128], kbg[:, a, :], xb[:, a, :], start=True, stop=True,
                  r=[rkb, rxb], w=[rpW], sig=(a == 3))
            nwt, rnw = nwtp.next()
            A("mul", nwt[:], pW[:].rearrange("p (h i) -> p h i", h=4), -1.0, r=[rpW], w=[rnw])
            if _KC <= 4:
                continue
            order = (0, 1) if rev == 0 else (1, 0)
            for sub in order:
                Rr = slice(sub * 64, (sub + 1) * 64)
                pV, rpV = pbank[4]
                for a in range(4):
                    T("matmul", pV[:, a * 128:(a + 1) * 128], xb[:, a, :], vb[:, a, :], start=True, stop=False,
                      r=[rxb, rvb], w=[rpV], sig=False)
                    T("matmul", pV[:, a * 128:(a + 1) * 128], nwt[:, a, :], Sb[:, h0 + a, :], start=False, stop=True,
                      r=[rnw, rSb], w=[rpV], sig=(a == 3))
                vnb, rvn = vnbp.next()
                act(vnb[Rr, :, :], pV[Rr, :].rearrange("p (h i) -> p h i", h=4), AF.Copy, r=[rpV], w=[rvn])
                if outp:
                    pO, rpO = pbank[6]
                    for a in range(4):
                        T("matmul", pO[:, a * 128:(a + 1) * 128], qdt[:, a, :], Sb[:, h0 + a, :], start=True,
                          stop=False, r=[rqd, rSb], w=[rpO], sig=False)
                        T("matmul", pO[:, a * 128:(a + 1) * 128], qkt[Rr, a, :], vnb[Rr, a, :], start=False,
                          stop=True, r=[rqt, rvn], w=[rpO], sig=(a == 3))
                    V("tensor_copy", gout[Rr, h0 * 128:(h0 + 4) * 128], pO[Rr, :], r=[rpO], w=[rgo])
                pU, rpU = pbank[5]
                for a in range(4):
                    T("matmul", pU[:, a * 128:(a + 1) * 128], kdc[Rr, a, :], vnb[Rr, a, :], start=True, stop=True,
                      r=[rkd, rvn], w=[rpU], sig=(a == 3))
                edc = ex[:, 16 + sub * 8 + h0:16 + sub * 8 + h0 + 4]
                V("tensor_tensor", S_[:, hs, :], S_[:, hs, :], bc(edc, [128, 4, 128], 2), ALU.mult, r=[rS, rex], w=[rS])
                V("tensor_tensor", S_[:, hs, :], S_[:, hs, :], pU[:].rearrange("p (h i) -> p h i", h=4), ALU.add,
                  r=[rS, rpU], w=[rS])
                act(Sb[:, hs, :], S_[:, hs, :], AF.Copy, r=[rS], w=[rSb])
        if outp:
            row = (c - 2) * 128
            DMA([(YG[d, row:row + 128, :], gout[:])], rgo, r=[rgo], w=[r_yg[d][c - 2]])

    for i in range(len(seqQ)):
        gdn_chunk(1, seqQ[i])
        if i < len(seqP):
            gdn_chunk(0, seqP[i])

    end_stage(ses)
    if _STOP == 'C':
        es.close()
        return nc
    ses = begin_stage()
    WO, r_wo = sb("WO", [128, 16, 1024], BF16)
    DMA([(WO[:], WOB.rearrange("(k p) n -> p k n", p=128))], r_wo, r=[r_wob], w=[r_wo])
    cstD, r_cstD = sb("cstD", [128, 8, 1024])
    DMA([(cstD[:], CST[:, 4:12, :])], r_cstD, r=[r_cst], w=[r_cstD])
    A2, B2, G1, G2m = cstD[:, 0, :], cstD[:, 1, :], cstD[:, 2, :], cstD[:, 3, :]
    gfin, gssd, ggdn, Dbc = cstD[:, 4, :], cstD[:, 5, :], cstD[:, 6, :], cstD[:, 7, :]
    r_gn = r_cstD
    r_modl = r_cstD
    r_A2 = r_cstD
    ldp = Pool("ldD", [128, 4, 1024], F32, 1)
    xtD = Pool("xtD", [128, 1024], BF16, 1)
    zgD = Pool("zgD", [128, 2048], BF16, 1)
    xrD = Pool("xrD", [128, 1024], F32, 1)
    t1p = Pool("t1D", [128, 1024], F32, 2)
    t2p = Pool("t2D", [128, 1024], F32, 2)
    ylp = Pool("yl", [128, 2048], BF16, 1)
    ylTp = Pool("ylT", [128, 16, 128], BF16, 1)
    x1p = Pool("x1", [128, TS, 1024], F32, 1)
    h2p = Pool("h2", [128, 1024], BF16, 1)
    h2Tp = Pool("h2T", [128, 8, TS * 128], BF16, 1)
    hidp = Pool("hid", [128, 32, TS * 128], BF16, 1)
    relp = Pool("rel", [128, 512], BF16, 2)
    w1pp = Pool("w1p", [128, 8, 512], BF16, 2)
    w2pp = Pool("w2p", [128, 32, 128], BF16, 2)
    outp_ = Pool("outD", [128, 1024], F32, 1)
    stD = Pool("stD", [128, 32], F32, 3)
    final_tokens = []
    W1B_v2 = W1B.rearrange("(k p) n -> p k n", p=128)
    W2B_v2 = W2B.rearrange("(f p) n -> p f n", p=128)
    nsup = (2 * NO) // TS
    for su in range(nsup):
        x1, rx1 = x1p.next()
        h2T, rh2T = h2Tp.next()
        for tsub in range(TS):
            tt = su * TS + tsub
            c = 2 + tt
            row = tt * 128
            ld, rld = ldp.next()
            DMA([(ld[:, 0, :], YS[0, row:row + 128, :])], rld, r=[r_ys[0][tt]], w=[rld])
            DMA([(ld[:, 1, :], YS[1, row:row + 128, :])], rld, r=[r_ys[1][tt]], w=[rld])
            DMA([(ld[:, 2, :], YG[0, row:row + 128, :])], rld, r=[r_yg[0][tt]], w=[rld])
            DMA([(ld[:, 3, :], YG[1, row:row + 128, :])], rld, r=[r_yg[1][tt]], w=[rld])
            xtk, rxt = xtD.next()
            DMA([(xtk[:], XTOK[c * 128:(c + 1) * 128, :])], rxt, r=[r_chunk[c]], w=[rxt])
            zg, rzg = zgD.next()
            DMA([(zg[:], ZG[row:row + 128, :])], rzg, r=[r_zg[tt]], w=[rzg])
            xr, rxr = xrD.next()
            DMA([(xr[:], xall[256 + row:256 + row + 128, :])], rxr, w=[rxr])
            st, rst = stD.next()
            G("memset", st[:], 0.0, w=[rst])
            t1, rt1 = t1p.next()
            t2, rt2 = t2p.next()
            yl, ryl = ylp.next()
            V("tensor_tensor", t1[:], ld[:, 0, :], ld[:, 1, :], ALU.add, r=[rld], w=[rt1])
            V("tensor_tensor", t2[:], xtk[:], Dbc, ALU.mult, r=[rxt, r_gn], w=[rt2])
            V("tensor_tensor", t1[:], t1[:], t2[:], ALU.add, r=[rt1, rt2], w=[rt1])
            V("tensor_tensor", t1[:], t1[:], zg[:, 0:1024], ALU.mult, r=[rt1, rzg], w=[rt1])
            for g in range(2):
                act(junk[:, 0:512], t1[:, g * 512:(g + 1) * 512], AF.Square, accum_out=st[:, g:g + 1], r=[rt1, rst],
                    w=[r_junk, rst])
            act(st[:, 2:4], st[:, 0:2], AF.Ln, scale=1.0 / 512, bias=EPS, r=[rst], w=[rst])
            act(st[:, 4:6], st[:, 2:4], AF.Exp, scale=-0.5, r=[rst], w=[rst])
            for g in range(2):
                V("scalar_tensor_tensor", out=yl[:, g * 512:(g + 1) * 512], in0=t1[:, g * 512:(g + 1) * 512],
                  scalar=st[:, 4 + g:5 + g], in1=gssd[:, g * 512:(g + 1) * 512], op0=ALU.mult, op1=ALU.mult,
                  r=[rt1, rst, r_gn], w=[ryl])
            t3, rt3 = t1p.next()
            t4, rt4 = t2p.next()
            V("tensor_tensor", t3[:], ld[:, 2, :], ld[:, 3, :], ALU.add, r=[rld], w=[rt3])
            V("tensor_tensor", t4[:], t3[:], t3[:], ALU.mult, r=[rt3], w=[rt4])
            V("tensor_reduce", out=st[:, 8:16], in_=t4[:].rearrange("p (h q) -> p h q", h=8), axis=AX.X, op=ALU.add,
              r=[rt4], w=[rst])
            act(st[:, 16:24], st[:, 8:16], AF.Ln, scale=1.0 / 128, bias=EPS, r=[rst], w=[rst])
            act(st[:, 24:32], st[:, 16:24], AF.Exp, scale=-0.5, r=[rst], w=[rst])
            V("tensor_tensor", t3[:].rearrange("p (h q) -> p h q", h=8), t3[:].rearrange("p (h q) -> p h q", h=8),
              bc(st[:, 24:32], [128, 8, 128], 2), ALU.mult, r=[rt3, rst], w=[rt3])
            V("tensor_tensor", t3[:], t3[:], ggdn, ALU.mult, r=[rt3, r_gn], w=[rt3])
            V("tensor_tensor", yl[:, 1024:2048], t3[:], zg[:, 1024:2048], ALU.mult, r=[rt3, rzg], w=[ryl])
            ylT, rylT = ylTp.next()
            for half in range(2):
                pb, rp = pbank[half]
                pv = pb[:].bitcast(BF16)
                for j in range(8):
                    k = half * 8 + j
                    T("transpose", pv[:, j * 128:(j + 1) * 128], yl[:, k * 128:(k + 1) * 128], identb[:], r=[ryl] + RC,
                      w=[rp], sig=(j == 7))
                if half == 0:
                    act(ylT[:, 0:8, :], pv.rearrange("p (k t) -> p k t", k=8), AF.Copy, r=[rp], w=[rylT])
                else:
                    V("tensor_copy", ylT[:, 8:16, :], pv.rearrange("p (k t) -> p k t", k=8), r=[rp], w=[rylT])
            for nh in range(2):
                pb, rp = pbank[2 + nh]
                for k in range(16):
                    T("matmul", pb[:], ylT[:, k, :], WO[:, k, nh * 512:(nh + 1) * 512], start=(k == 0), stop=(k == 15),
                      r=[rylT, r_wo], w=[rp], sig=(k == 15))
                cs = slice(nh * 512, (nh + 1) * 512)
                V("tensor_tensor", t2[:, cs], pb[:], G1[:, cs], ALU.mult, r=[rp, r_modl], w=[rt2])
                V("tensor_tensor", x1[:, tsub, cs], xr[:, cs], t2[:, cs], ALU.add, r=[rxr, rt2], w=[rx1])
            act(junk[:], x1[:, tsub, :], AF.Square, accum_out=st[:, 6:7], r=[rx1, rst], w=[r_junk, rst])
            act(st[:, 7:8], st[:, 6:7], AF.Ln, scale=1.0 / 1024, bias=EPS, r=[rst], w=[rst])
            act(st[:, 6:7], st[:, 7:8], AF.Exp, scale=-0.5, r=[rst], w=[rst])
            h2, rh2 = h2p.next()
            V("scalar_tensor_tensor", out=t4[:], in0=x1[:, tsub, :], scalar=st[:, 6:7], in1=A2, op0=ALU.mult,
              op1=ALU.mult, r=[rx1, rst, r_A2], w=[rt4])
            V("tensor_tensor", h2[:], t4[:], B2, ALU.add, r=[rt4, r_modl], w=[rh2])
            pb, rp = pbank[4]
            pv = pb[:].bitcast(BF16)
            for k in range(8):
                T("transpose", pv[:, k * 128:(k + 1) * 128], h2[:, k * 128:(k + 1) * 128], identb[:], r=[rh2] + RC,
                  w=[rp], sig=(k == 7))
            act(h2T[:, :, tsub * 128:(tsub + 1) * 128], pv.rearrange("p (k t) -> p k t", k=8), AF.Copy, r=[rp], w=[rh2T])
        NTT = TS * 128
        hid, rhid = hidp.next()
        for pi in range(8):
            w1t, rw1 = w1pp.next()
            DMA([(w1t[:], W1B_v2[:, :, pi * 512:(pi + 1) * 512])], rw1, r=[r_w1b], w=[rw1])
            for fj in range(4):
                f = pi * 4 + fj
                pb, rp = pbank[5 + (f % 2)]
                for k in range(8):
                    T("matmul", pb[:, 0:NTT], w1t[:, k, fj * 128:(fj + 1) * 128], h2T[:, k, :], start=(k == 0),
                      stop=(k == 7), r=[rw1, rh2T], w=[rp], sig=(k == 7))
                rel, rrel = relp.next()
                act(rel[:, 0:NTT], pb[:, 0:NTT], AF.Relu, r=[rp], w=[rrel])
                V("tensor_tensor", hid[:, f, :], rel[:, 0:NTT], rel[:, 0:NTT], ALU.mult, r=[rrel], w=[rhid])
        x2, rx2 = x1, rx1
        for pi in range(8):
            w2t, rw2 = w2pp.next()
            DMA([(w2t[:], W2B_v2[:, :, pi * 128:(pi + 1) * 128])], rw2, r=[r_w2b], w=[rw2])
            cs = slice(pi * 128, (pi + 1) * 128)
            for tsub in range(TS):
                pb, rp = pbank[2 + (tsub % 2)]
                for f in range(32):
                    T("matmul", pb[:, 0:128], hid[:, f, tsub * 128:(tsub + 1) * 128], w2t[:, f, :], start=(f == 0),
                      stop=(f == 31), r=[rhid, rw2], w=[rp], sig=(f == 31))
                t2, rt2 = t2p.next()
                V("tensor_tensor", t2[:, 0:128], pb[:, 0:128], G2m[:, cs], ALU.mult, r=[rp, r_modl], w=[rt2])
                V("tensor_tensor", x2[:, tsub, cs], x2[:, tsub, cs], t2[:, 0:128], ALU.add, r=[rx2, rt2], w=[rx2])
        for tsub in range(TS):
            tt = su * TS + tsub
            row = tt * 128
            st, rst = stD.next()
            G("memset", st[:], 0.0, w=[rst])
            act(junk[:], x2[:, tsub, :], AF.Square, accum_out=st[:, 0:1], r=[rx2, rst], w=[r_junk, rst])
            act(st[:, 1:2], st[:, 0:1], AF.Ln, scale=1.0 / 1024, bias=EPS, r=[rst], w=[rst])
            act(st[:, 2:3], st[:, 1:2], AF.Exp, scale=-0.5, r=[rst], w=[rst])
            ot, rot = outp_.next()
            V("scalar_tensor_tensor", out=ot[:], in0=x2[:, tsub, :], scalar=st[:, 2:3], in1=gfin, op0=ALU.mult,
              op1=ALU.mult, r=[rx2, rst, r_gn], w=[rot])
            final_tokens.append(DMA([(out[row:row + 128, :], ot[:])], rot, r=[rot]))

    end_stage(ses)
    es.close()
    return nc


_CACHE = {}


def _prep_core(b, hf, x, c, ctx, c_ctx, w_mod, b_mod, norm1_g, w_in, ssd_conv_w, ssd_conv_b, ssd_dt_bias,
               ssd_A_log, ssd_D, ssd_norm_g, gdn_conv_w, gdn_dt_bias, gdn_A_log, gdn_norm_g, w_out, norm2_g,
               w_mlp1, w_mlp2, final_g):
    S = x.shape[1]
    H = S // 2
    f32 = np.float32
    if hf == 0:
        own, far, cx = x[b, :H], x[b, H:], ctx[b]
        dP, dQ = 0, 1
    else:
        own, far, cx = x[b, H:][::-1], x[b, :H][::-1], ctx[b][::-1]
        dP, dQ = 1, 0
    xall = np.ascontiguousarray(np.concatenate([cx, own, far], 0), dtype=f32)
    cvec = np.concatenate([c[b].reshape(8, 128).T, c_ctx.reshape(8, 128).T], 1).astype(f32)
    gains = np.stack([norm1_g[0], norm2_g[0], final_g, ssd_norm_g[0], np.tile(gdn_norm_g[0], 8),
                      np.repeat(ssd_D[0], 64)], 0)
    gains = np.ascontiguousarray(np.broadcast_to(gains[None], (128, 6, 1024)), dtype=f32)
    o_z, o_xbc, o_dt, o_qkv, o_gate, o_a, o_b = 0, 1024, 2304, 2336, 5408, 6432, 6448
    xs = np.arange(o_xbc, o_xbc + 1024)
    Bc = np.arange(o_xbc + 1024, o_xbc + 1152)
    Cc = np.arange(o_xbc + 1152, o_xbc + 1280)
    q = np.arange(o_qkv, o_qkv + 1024)
    k = np.arange(o_qkv + 1024, o_qkv + 2048)
    v = np.arange(o_qkv + 2048, o_qkv + 3072)
    dt = lambda d: np.arange(o_dt + d * 16, o_dt + (d + 1) * 16)
    aa = lambda d: np.arange(o_a + d * 8, o_a + (d + 1) * 8)
    bb = lambda d: np.arange(o_b + d * 8, o_b + (d + 1) * 8)
    z = np.arange(o_z, o_z + 1024)
    gate = np.arange(o_gate, o_gate + 1024)
    perm = np.concatenate([xs, Bc, k, v, Cc, q, dt(dP), dt(dQ), aa(dP), aa(dQ), bb(dP), bb(dQ), z, gate])
    assert perm.shape[0] == 6464
    win = np.ascontiguousarray(w_in[0][:, perm], dtype=f32)
    sw, gw = ssd_conv_w[0], gdn_conv_w[0]
    if hf == 1:
        sw, gw = sw[::-1], gw[::-1]
    cw_cols = np.concatenate([sw[:, 0:1024], sw[:, 1024:1152], gw[:, 1024:2048], gw[:, 2048:3072],
                              sw[:, 1152:1280], gw[:, 0:1024]], 1)
    convw = np.ascontiguousarray(cw_cols.reshape(5, 34, 128).transpose(2, 1, 0), dtype=f32)
    sbias = ssd_conv_b[0]
    cb_cols = np.concatenate([sbias[0:1024], sbias[1024:1152], np.zeros(2048, f32), sbias[1152:1280],
                              np.zeros(1024, f32)])
    convb = np.ascontiguousarray(cb_cols.reshape(34, 128).T, dtype=f32)
    sp = np.concatenate([ssd_dt_bias[0, dP], ssd_dt_bias[0, dQ], gdn_dt_bias[0, dP], gdn_dt_bias[0, dQ],
                         ssd_A_log[0, dP], ssd_A_log[0, dQ], gdn_A_log[0, dP], gdn_A_log[0, dQ]])
    smallp = np.ascontiguousarray(np.broadcast_to(sp[None], (128, 96)), dtype=f32)
    return {"xall": xall, "cvec": np.ascontiguousarray(cvec), "wmod": np.ascontiguousarray(w_mod[0], dtype=f32),
            "bmod": np.ascontiguousarray(b_mod[0][None], dtype=f32), "gains": gains, "w_in": win, "convw": convw,
            "convb": convb, "smallp": smallp, "w_out": np.ascontiguousarray(w_out[0], dtype=f32),
            "w1": np.ascontiguousarray(w_mlp1[0], dtype=f32), "w2": np.ascontiguousarray(w_mlp2[0], dtype=f32)}


def kernel(**inp):
    inp = {k: np.asarray(v) for k, v in inp.items()}
    x = inp["x"]
    Bn, S, D = x.shape
    assert Bn == 4 and D == 1024
    H = S // 2
    NO = H // 256
    if NO not in _CACHE:
        _CACHE[NO] = build(NO)
    nc = _CACHE[NO]
    in_maps = []
    for core in range(8):
        in_maps.append(_prep_core(core // 2, core % 2, **inp))
    res = run_bass_kernel_spmd(nc, in_maps, core_ids=list(range(8)))
    outf = np.empty((Bn, S, D), np.float32)
    for core in range(8):
        b, hf = core // 2, core % 2
        o = np.asarray(res.results[core]["out"], dtype=np.float32)
        if hf == 0:
            outf[b, :H] = o
        else:
            outf[b, H:] = o[::-1]
    return outf
```

```python
import math
import numpy as np
from contextlib import ExitStack
import concourse.bass as bass
import concourse.mybir as mybir
from concourse.bass_utils import run_bass_kernel_spmd

F32 = mybir.dt.float32
BF16 = mybir.dt.bfloat16
ALU = mybir.AluOpType
AF = mybir.ActivationFunctionType
AX = mybir.AxisListType
EPS = 1e-6
import os as _os
_STOP = _os.environ.get("KSTOP", "")
_KC = int(_os.environ.get("KC", "9"))
NEG = -30000.0


class Res:
    __slots__ = ("name", "last_w", "readers", "dsem", "dcnt")

    def __init__(self, name=""):
        self.name = name
        self.last_w = None
        self.readers = []
        self.dsem = None
        self.dcnt = 0


class Prog:
    ENGS = ("tensor", "vector", "scalar", "gpsimd", "sync")
    NDS = 48

    def __init__(self, nc, es):
        self.nc = nc
        self.ops = {e: [] for e in self.ENGS}
        self.nops = {e: 0 for e in self.ENGS}
        self.stage_start = {e: 0 for e in self.ENGS}
        self.semval = {e: 0 for e in self.ENGS}
        self.val = {e: {} for e in self.ENGS}
        self.maxw = {e: {} for e in self.ENGS}
        self.sems = {}
        for e in self.ENGS:
            self.sems[("E", e)] = es.enter_context(nc.semaphore("se_" + e))
        for i in range(self.NDS):
            self.sems[("D", i)] = es.enter_context(nc.semaphore("sd_%d" % i))
        self.dfree = list(range(self.NDS))
        self.dcount = {i: 0 for i in range(self.NDS)}
        self.dres = []

    def _raw(self, reads, writes):
        raw = []
        for r in reads:
            if r.last_w is not None:
                raw.append(r.last_w)
        for w in writes:
            if w.last_w is not None:
                raw.append(w.last_w)
            raw.extend(w.readers)
        return raw

    def _commit(self, tok, reads, writes):
        for r in reads:
            r.readers.append(tok)
            if len(r.readers) > 64:
                best = {}
                for t in r.readers:
                    k = t[:2]
                    if k not in best or best[k][2] < t[2]:
                        best[k] = t
                r.readers = list(best.values())
        for w in writes:
            w.last_w = tok
            w.readers = []

    def op(self, eng, fn, reads=(), writes=(), signal=True):
        idx = self.nops[eng]
        self.nops[eng] += 1
        self.ops[eng].append({"fn": fn, "raw": self._raw(reads, writes), "idx": idx})
        tok = ("E", eng, idx)
        self._commit(tok, reads, writes)
        return tok

    def dma(self, queue, pairs, sres, reads=(), writes=()):
        raw = self._raw(reads, writes)
        if sres.dsem is None:
            sres.dsem = self.dfree.pop(0)
            self.dres.append(sres)
        i = sres.dsem
        self.dcount[i] += 16 * len(pairs)
        tok = ("D", i, self.dcount[i])
        self.ops[queue].append({"dma": pairs, "raw": raw, "inc": ("D", i), "idx": None})
        self._commit(tok, reads, writes)
        return tok

    def barrier(self):
        for e in self.ENGS:
            raw = []
            for e2 in self.ENGS:
                if e2 != e and self.nops[e2] > self.stage_start[e2]:
                    raw.append(("E", e2, self.nops[e2] - 1))
            for i in range(self.NDS):
                if self.dcount[i] > 0:
                    raw.append(("D", i, self.dcount[i]))
            self.ops[e].append({"fn": None, "raw": raw, "idx": None})
        for r in self.dres:
            self.dfree.append(r.dsem)
            r.dsem = None
        self.dres = []

    def flush(self):
        nc = self.nc
        sems = self.sems
        needed = {e: set() for e in self.ENGS}
        for e in self.ENGS:
            for rec in self.ops[e]:
                w = {}
                for t in rec["raw"]:
                    if t[0] == "E":
                        _, e2, idx2 = t
                        if idx2 < self.stage_start[e2]:
                            continue
                        if e2 == e and e == "tensor":
                            continue
                        k = ("E", e2)
                        if idx2 > self.maxw[e].get(k, -1):
                            self.maxw[e][k] = idx2
                            w[k] = idx2
                    else:
                        _, i, v = t
                        k = ("D", i)
                        if v > self.maxw[e].get(k, 0):
                            self.maxw[e][k] = v
                            w[k] = v
                rec["w"] = w
                for k, v in w.items():
                    if k[0] == "E":
                        needed[k[1]].add(v)
        for e in self.ENGS:
            c = self.semval[e]
            for rec in self.ops[e]:
                rec["sig"] = False
                if rec.get("fn") is not None and rec["idx"] in needed[e]:
                    c += 1
                    rec["sig"] = True
                    self.val[e][rec["idx"]] = c
            self.semval[e] = c
        with nc.Block() as block:
            def run(engname):
                def body(eng):
                    for rec in self.ops[engname]:
                        for k, v in rec["w"].items():
                            if k[0] == "E":
                                eng.wait_ge(sems[k], self.val[k[1]][v])
                            else:
                                eng.wait_ge(sems[k], v)
                        if "dma" in rec:
                            for (o, i) in rec["dma"]:
                                eng.dma_start(out=o, in_=i).then_inc(sems[rec["inc"]], 16)
                        elif rec["fn"] is not None:
                            ins = rec["fn"](eng)
                            if rec["sig"]:
                                ins.then_inc(sems[("E", engname)], 1)
                return body

            block.tensor(run("tensor"))
            block.vector(run("vector"))
            block.scalar(run("scalar"))
            block.gpsimd(run("gpsimd"))
            block.sync(run("sync"))
        self.ops = {e: [] for e in self.ENGS}
        for e in self.ENGS:
            self.stage_start[e] = self.nops[e]


def build(NO):
    nc = bass.Bass("TRN2", target_bir_lowering=False)
    NT = 1 + 2 * NO
    TOK = NT * 256
    NCH = 2 * NT
    NOWN = NO * 256
    TS = 2

    def din(name, shape):
        return nc.dram_tensor(name, shape, F32, kind="ExternalInput").ap()

    xall = din("xall", [TOK, 1024])
    cvec = din("cvec", [128, 16])
    wmod = din("wmod", [1024, 6144])
    bmod = din("bmod", [1, 6144])
    gains = din("gains", [128, 6, 1024])
    w_in = din("w_in", [1024, 6464])
    convw = din("convw", [128, 34, 5])
    convb = din("convb", [128, 34])
    smallp = din("smallp", [128, 96])
    w_out = din("w_out", [2048, 1024])
    w1 = din("w1", [1024, 4096])
    w2 = din("w2", [4096, 1024])
    out = nc.dram_tensor("out", [NOWN, 1024], F32, kind="ExternalOutput").ap()

    def dscr(name, shape, dt):
        return nc.dram_tensor(name, shape, dt, kind="Internal").ap()

    XTOK = dscr("XTOK", [TOK, 1024], BF16)
    KTOK = dscr("KTOK", [TOK, 1024], BF16)
    VTOK = dscr("VTOK", [TOK, 1024], BF16)
    BTOK = dscr("BTOK", [TOK, 128], BF16)
    FMS = dscr("FMS", [NCH, 128, 18, 128], BF16)
    SM = dscr("SM", [TOK, 128], F32)
    ZG = dscr("ZG", [NOWN, 2048], BF16)
    YS = dscr("YS", [2, NOWN, 1024], F32)
    YG = dscr("YG", [2, NOWN, 1024], F32)
    W1B = dscr("W1B", [1024, 4096], BF16)
    W2B = dscr("W2B", [4096, 1024], BF16)
    r_chunk = [Res("ch%d" % c) for c in range(NCH)]
    r_zg = [Res() for _ in range(2 * NO)]
    r_ys = [[Res() for _ in range(2 * NO)] for _ in range(2)]
    r_yg = [[Res() for _ in range(2 * NO)] for _ in range(2)]
    r_w1b = Res("w1b")
    r_w2b = Res("w2b")

    es = ExitStack()
    P = Prog(nc, es)
    cur = [es]

    def sb(name, shape, dt=F32):
        return cur[0].enter_context(nc.sbuf_tensor(name, shape, dt)), Res(name)

    class Pool:
        def __init__(self, name, shape, dt, n):
            self.t = [sb("%s_%d" % (name, i), shape, dt) for i in range(n)]
            self.i = 0

        def next(self):
            t = self.t[self.i % len(self.t)]
            self.i += 1
            return t

    def V(m, *a, r=(), w=(), **kw):
        return P.op("vector", lambda e: getattr(e, m)(*a, **kw), r, w)

    def A(m, *a, r=(), w=(), **kw):
        return P.op("scalar", lambda e: getattr(e, m)(*a, **kw), r, w)

    def G(m, *a, r=(), w=(), **kw):
        return P.op("gpsimd", lambda e: getattr(e, m)(*a, **kw), r, w)

    def T(m, *a, r=(), w=(), sig=True, **kw):
        return P.op("tensor", lambda e: getattr(e, m)(*a, **kw), r, w, signal=sig)

    def act(out_, in_, func, r=(), w=(), **kw):
        return A("activation", out=out_, in_=in_, func=func, r=r, w=w, **kw)

    dq = [0]

    def DMA(pairs, sres, r=(), w=(), q=None):
        if q is None:
            q = "sync"
            dq[0] += 1
        return P.dma(q, pairs, sres, r, w)

    def bc(ap, shape, axis):
        return ap.unsqueeze(axis).to_broadcast(shape)

    def end_stage(ses):
        P.barrier()
        P.flush()
        ses.close()
        cur[0] = es

    def begin_stage():
        ses = ExitStack()
        cur[0] = ses
        return ses

    pbank = []
    for i in range(8):
        t = es.enter_context(nc.psum_tensor("pb%d" % i, [128, 512], F32))
        pbank.append((t, Res("pb%d" % i)))

    identf, r_c = sb("identf", [128, 128])
    identb, _ = sb("identb", [128, 128], BF16)
    onesf, _ = sb("onesf", [128, 128])
    negonesf, _ = sb("negonesf", [128, 128])
    onesb, _ = sb("onesb", [128, 128], BF16)
    ones1, _ = sb("ones1", [1, 128])
    junk, r_junk = sb("junk", [128, 1024], BF16)
    RC = [r_c]
    G("memset", onesf[:], 1.0, w=RC)
    G("memset", negonesf[:], -1.0, w=RC)
    G("memset", ones1[:], 1.0, w=RC)
    G("memset", identf[:], 1.0, w=RC)
    G("affine_select", out=identf[:], in_=identf[:], pattern=[[-1, 128]], compare_op=ALU.is_equal,
      fill=0.0, base=0, channel_multiplier=1, r=RC, w=RC)
    V("tensor_copy", identb[:], identf[:], r=RC, w=RC)
    V("tensor_copy", onesb[:], onesf[:], r=RC, w=RC)

    def tri(name, cmp_ge_free_minus_part, strict, blockdiag, fillv=0.0, onev=1.0, dt=F32, rep=1):
        tf, _ = sb(name + "_f", [128, 128])
        G("memset", tf[:], onev, w=RC)
        if cmp_ge_free_minus_part:
            G("affine_select", out=tf[:], in_=tf[:], pattern=[[1, 128]], compare_op=ALU.is_ge, fill=fillv,
              base=-strict, channel_multiplier=-1, r=RC, w=RC)
        else:
            G("affine_select", out=tf[:], in_=tf[:], pattern=[[-1, 128]], compare_op=ALU.is_ge, fill=fillv,
              base=-strict, channel_multiplier=1, r=RC, w=RC)
        if blockdiag:
            G("memset", tf[0:64, 64:128], fillv, r=RC, w=RC)
            G("memset", tf[64:128, 0:64], fillv, r=RC, w=RC)
        if dt == F32 and rep == 1:
            return tf
        t2, _ = sb(name, [128, rep, 128], dt)
        for a in range(rep):
            V("tensor_copy", t2[:, a, :], tf[:], r=RC, w=RC)
        return t2

    CST = dscr("CST", [128, 12, 1024], F32)
    WOB = dscr("WOB", [2048, 1024], BF16)
    r_cst = Res("cst")
    r_wob = Res("wob")
    ses = begin_stage()
    wstage = Pool("wst", [128, 8, 512], F32, 2)
    gn, r_gn = sb("gn", [128, 6, 1024])
    DMA([(gn[:], gains[:, :, :])], r_gn, w=[r_gn])
    modl, r_modl = sb("modl", [128, 6144])
    modc, r_modc = sb("modc", [128, 2048])
    cv, r_cv = sb("cv", [128, 16])
    scb, r_scb = sb("scb", [128, 16, 128])
    bmp = Pool("bmt", [1, 512], F32, 2)
    DMA([(cv[:], cvec[:, :])], r_cv, w=[r_cv])
    act(cv[:], cv[:], AF.Silu, r=[r_cv], w=[r_cv])
    V("tensor_copy", scb[:], bc(cv[:], [128, 16, 128], 2), r=[r_cv], w=[r_scb])
    wmod_v = wmod.rearrange("(k p) n -> p k n", p=128)
    for ng in range(12):
        wt, rw = wstage.next()
        DMA([(wt[:], wmod_v[:, :, ng * 512:(ng + 1) * 512])], rw, w=[rw])
        bmt, r_bmt = bmp.next()
        DMA([(bmt[:], bmod[:, ng * 512:(ng + 1) * 512])], r_bmt, w=[r_bmt])
        for which in range(2 if ng < 4 else 1):
            pb, rp = pbank[(ng + which) % 8]
            for k in range(8):
                T("matmul", pb[:], scb[:, which * 8 + k, :], wt[:, k, :], start=(k == 0), stop=False,
                  r=[rw, r_scb], w=[rp], sig=False)
            T("matmul", pb[:], ones1[0:1, :], bmt[0:1, :], start=False, stop=True,
              r=[r_bmt] + RC, w=[rp])
            if which == 0:
                V("tensor_copy", modl[:, ng * 512:(ng + 1) * 512], pb[:], r=[rp], w=[r_modl])
            else:
                V("tensor_copy", modc[:, ng * 512:(ng + 1) * 512], pb[:], r=[rp], w=[r_modc])
    ctmp = Pool("ctmp", [128, 1024], F32, 2)
    for slot, (src, gi_) in {0: (modl[:, 1024:2048], 0), 2: (modc[:, 1024:2048], 0), 4: (modl[:, 4096:5120], 1)}.items():
        ct, rct = ctmp.next()
        V("scalar_tensor_tensor", out=ct[:], in0=src, scalar=1.0, in1=gn[:, gi_, :], op0=ALU.add,
          op1=ALU.mult, r=[r_modl, r_modc, r_gn], w=[rct])
        DMA([(CST[:, slot, :], ct[:])], rct, r=[rct], w=[r_cst])
    DMA([(CST[:, 1, :], modl[:, 0:1024]), (CST[:, 5, :], modl[:, 3072:4096]), (CST[:, 6, :], modl[:, 2048:3072]),
         (CST[:, 7, :], modl[:, 5120:6144])], r_modl, r=[r_modl], w=[r_cst])
    DMA([(CST[:, 3, :], modc[:, 0:1024])], r_modc, r=[r_modc], w=[r_cst])
    DMA([(CST[:, 8, :], gn[:, 2, :]), (CST[:, 9, :], gn[:, 3, :]), (CST[:, 10, :], gn[:, 4, :]),
         (CST[:, 11, :], gn[:, 5, :])], r_gn, r=[r_gn], w=[r_cst])
    wcast = Pool("wcast", [128, 8, 512], BF16, 2)
    w1_v = w1.rearrange("(k p) n -> p k n", p=128)
    W1B_v = W1B.rearrange("(k p) n -> p k n", p=128)
    for pi in range(8):
        wt, rw = wstage.next()
        DMA([(wt[:], w1_v[:, :, pi * 512:(pi + 1) * 512])], rw, w=[rw])
        wc, rc_ = wcast.next()
        G("tensor_copy", wc[:], wt[:], r=[rw], w=[rc_])
        DMA([(W1B_v[:, :, pi * 512:(pi + 1) * 512], wc[:])], rc_, r=[rc_], w=[r_w1b])
    for (srcw, dstw, rdst, nfc) in ((w2, W2B, r_w2b, 32), (w_out, WOB, r_wob, 16)):
        s_v = srcw.rearrange("(f p) n -> p f n", p=128)
        d_v = dstw.rearrange("(f p) n -> p f n", p=128)
        for pi in range(nfc // 4):
            wt, rw = wstage.next()
            wtv = wt[:].rearrange("p k n -> p (k n)").rearrange("p (f n) -> p f n", f=4)
            DMA([(wtv, s_v[:, pi * 4:(pi + 1) * 4, :])], rw, w=[rw])
            wc, rc_ = wcast.next()
            wcv = wc[:].rearrange("p k n -> p (k n)").rearrange("p (f n) -> p f n", f=4)
            A("activation", out=wcv, in_=wtv, func=AF.Copy, r=[rw], w=[rc_])
            DMA([(d_v[:, pi * 4:(pi + 1) * 4, :], wcv)], rc_, r=[rc_], w=[rdst])
    end_stage(ses)
    if _STOP == '0':
        es.close()
        return nc

    ses = begin_stage()
    WIN, r_win = sb("WIN", [128, 8, 6464], BF16)
    wstage = Pool("wstA", [128, 8, 512], F32, 1)
    win_v = w_in.rearrange("(k p) n -> p k n", p=128)
    for pi in range(13):
        c0 = pi * 512
        c1 = min(6464, c0 + 512)
        wt, rw = wstage.next()
        DMA([(wt[:, :, 0:c1 - c0], win_v[:, :, c0:c1])], rw, w=[rw])
        if pi % 2 == 0:
            act(WIN[:, :, c0:c1], wt[:, :, 0:c1 - c0], AF.Copy, r=[rw], w=[r_win])
        else:
            V("tensor_copy", WIN[:, :, c0:c1], wt[:, :, 0:c1 - c0], r=[rw], w=[r_win])
    cstA, r_cstA = sb("cstA", [128, 4, 1024])
    DMA([(cstA[:], CST[:, 0:4, :])], r_cstA, r=[r_cst], w=[r_cstA])
    A1, B1, A1c, B1c = cstA[:, 0, :], cstA[:, 1, :], cstA[:, 2, :], cstA[:, 3, :]
    cw, r_cw = sb("cw", [128, 34, 5])
    cbt, r_cb = sb("cbt", [128, 34])
    DMA([(cw[:], convw[:, :, :])], r_cw, w=[r_cw])
    DMA([(cbt[:], convb[:, :])], r_cb, w=[r_cb])
    spp, r_spp = sb("spp", [128, 96])
    coef, r_coef = sb("coef", [128, 48])
    DMA([(spp[:], smallp[:, :])], r_spp, w=[r_spp])
    act(coef[:], spp[:, 48:96], AF.Exp, r=[r_spp], w=[r_coef])
    V("tensor_scalar_mul", coef[:], coef[:], -1.0, r=[r_coef], w=[r_coef])

    xpool = Pool("xt", [128, 2, 1024], F32, 1)
    statp = Pool("stat", [128, 8], F32, 3)
    hfp = Pool("hf", [128, 1024], F32, 1)
    hbp = Pool("hb", [128, 2, 1024], BF16, 1)
    hTp = Pool("hT", [128, 8, 256], BF16, 1)
    caccp = Pool("cacc", [128, 256], F32, 3)
    fmbp = Pool("fmb", [128, 34, 256], BF16, 1)
    qkfp = Pool("qkf", [128, 16, 256], BF16, 1)
    sqbp = Pool("sqb", [128, 512], BF16, 2)
    lnqp = Pool("lnq", [128, 512], F32, 1)
    rnqp = Pool("rnq", [128, 512], F32, 1)
    tokp = Pool("tok", [128, 1024], BF16, 2)
    btkp = Pool("btkA", [128, 128], BF16, 2)
    smop = Pool("smo", [128, 96], F32, 2)
    sptp = Pool("spt", [128, 64], F32, 2)
    zgp = Pool("zgt", [128, 2048], BF16, 1)
    FMCOL = list(range(34))
    LNQS = math.log(128.0 ** -0.5)

    for ti in range(NT):
        kind = "ctx" if ti == 0 else ("own" if ti <= NO else "far")
        RL = 256 if kind == "ctx" else 64
        nfm = 34 if kind == "own" else 25
        Abc, Bbc = (A1c, B1c) if kind == "ctx" else (A1, B1)
        rAB = [r_cstA]
        tok0 = ti * 256
        xt, rx = xpool.next()
        DMA([(xt[:], xall[tok0:tok0 + 256, :].rearrange("(s p) d -> p s d", p=128))], rx, w=[rx])
        st, rst = statp.next()
        G("memset", st[:], 0.0, w=[rst])
        for s in range(2):
            act(junk[:], xt[:, s, :], AF.Square, accum_out=st[:, s:s + 1], r=[rx, rst], w=[r_junk, rst])
        act(st[:, 2:4], st[:, 0:2], AF.Ln, scale=1.0 / 1024, bias=EPS, r=[rst], w=[rst])
        act(st[:, 4:6], st[:, 2:4], AF.Exp, scale=-0.5, r=[rst], w=[rst])
        hb, rhb = hbp.next()
        for s in range(2):
            hf, rhf = hfp.next()
            V("scalar_tensor_tensor", out=hf[:], in0=xt[:, s, :], scalar=st[:, 4 + s:5 + s], in1=Abc,
              op0=ALU.mult, op1=ALU.mult, r=[rx, rst] + rAB, w=[rhf])
            V("tensor_tensor", hb[:, s, :], hf[:], Bbc, ALU.add, r=[rhf] + rAB, w=[rhb])
        hT, rhT = hTp.next()
        for half in range(2):
            pb, rp = pbank[half]
            pv = pb[:].bitcast(BF16).rearrange("p (k t) -> p k t", k=4)
            for kk in range(4):
                k = half * 4 + kk
                for s in range(2):
                    T("transpose", pv[:, kk, s * 128:(s + 1) * 128], hb[:, s, k * 128:(k + 1) * 128], identb[:],
                      r=[rhb] + RC, w=[rp], sig=(kk == 3 and s == 1))
            if half == 0:
                act(hT[:, 0:4, :], pv, AF.Copy, r=[rp], w=[rhT])
            else:
                V("tensor_copy", hT[:, 4:8, :], pv, r=[rp], w=[rhT])
        fmb, rfm = fmbp.next()
        qkf, rqk = qkfp.next()
        for ci in range(nfm):
            pb, rp = pbank[2 + (ci % 2)]
            half = (ci // 2) % 2
            pc = pb[:, half * 256:(half + 1) * 256]
            for k in range(8):
                T("matmul", pc, WIN[:, k, ci * 128:(ci + 1) * 128], hT[:, k, :], start=(k == 0), stop=(k == 7),
                  r=[r_win, rhT], w=[rp], sig=(k == 7))
            ca, rca = caccp.next()
            pv = pc.rearrange("p (r l) -> p r l", l=RL)
            av = ca[:].rearrange("p (r l) -> p r l", l=RL)
            V("tensor_scalar", av, pv, cw[:, ci, 2:3], cbt[:, ci:ci + 1], ALU.mult, ALU.add,
              r=[rp, r_cw, r_cb], w=[rca])
            for j in (0, 1, 3, 4):
                sft = j - 2
                if sft > 0:
                    o_sl, i_sl = slice(0, RL - sft), slice(sft, RL)
                else:
                    o_sl, i_sl = slice(-sft, RL), slice(0, RL + sft)
                V("scalar_tensor_tensor", out=av[:, :, o_sl], in0=pv[:, :, i_sl], scalar=cw[:, ci, j:j + 1],
                  in1=av[:, :, o_sl], op0=ALU.mult, op1=ALU.add, r=[rp, r_cw, rca], w=[rca])
            isk = 9 <= ci <= 16
            isq = 26 <= ci <= 33
            if isk or isq:
                qi = (ci - 9) if isk else (8 + ci - 26)
                act(qkf[:, qi, :], ca[:], AF.Silu, r=[rca], w=[rqk])
            else:
                act(fmb[:, ci, :], ca[:], AF.Silu, r=[rca], w=[rfm])
        if kind == "own":
            for s in range(2):
                zg, rzg = zgp.next()
                for cg in range(4):
                    pb, rp = pbank[7]
                    for k in range(8):
                        T("matmul", pb[:], hT[:, k, s * 128:(s + 1) * 128],
                          WIN[:, k, 4416 + cg * 512:4416 + (cg + 1) * 512], start=(k == 0), stop=(k == 7),
                          r=[r_win, rhT], w=[rp], sig=(k == 7))
                    act(zg[:, cg * 512:(cg + 1) * 512], pb[:], AF.Silu, r=[rp], w=[rzg])
                row = (ti - 1) * 256 + s * 128
                DMA([(ZG[row:row + 128, :], zg[:])], rzg, r=[rzg], w=[r_zg[(ti - 1) * 2 + s]])
        ngrp = 8 if kind == "own" else 4
        for pr in range(ngrp):
            pb, rp = pbank[4]
            sqb, rsq = sqbp.next()
            for u in range(2):
                qi = pr * 2 + u
                act(sqb[:, u * 256:(u + 1) * 256], qkf[:, qi, :], AF.Square, r=[rqk], w=[rsq])
            T("matmul", pb[:], onesb[:], sqb[:], start=True, stop=True, r=[rsq] + RC, w=[rp])
            lnq, rln = lnqp.next()
            rnq, rrn = rnqp.next()
            act(lnq[:], pb[:], AF.Ln, bias=EPS, r=[rp], w=[rln])
            act(rnq[:], lnq[:], AF.Exp, scale=-0.5, bias=(LNQS if pr >= 4 else 0.0), r=[rln], w=[rrn])
            for u in range(2):
                qi = pr * 2 + u
                ci = (9 + qi) if qi < 8 else (26 + qi - 8)
                V("tensor_tensor", fmb[:, ci, :], qkf[:, qi, :], rnq[:, u * 256:(u + 1) * 256], ALU.mult,
                  r=[rqk, rrn], w=[rfm])
        for s in range(2):
            pb, rp = pbank[7]
            for k in range(8):
                T("matmul", pb[:, 0:64], hT[:, k, s * 128:(s + 1) * 128], WIN[:, k, 4352:4416], start=(k == 0),
                  stop=(k == 7), r=[r_win, rhT], w=[rp], sig=(k == 7))
            spt, rsp = sptp.next()
            smo, rsm = smop.next()
            V("tensor_tensor", spt[:, 0:48], pb[:, 0:48], spp[:, 0:48], ALU.add, r=[rp, r_spp], w=[rsp])
            act(spt[:, 48:64], pb[:, 48:64], AF.Exp, scale=-1.0, r=[rp], w=[rsp])
            act(spt[:, 0:48], spt[:, 0:48], AF.Exp, r=[rsp], w=[rsp])
            act(spt[:, 0:48], spt[:, 0:48], AF.Ln, bias=1.0, r=[rsp], w=[rsp])
            V("tensor_copy", smo[:, 0:32], spt[:, 0:32], r=[rsp], w=[rsm])
            V("tensor_tensor", smo[:, 32:80], spt[:, 0:48], coef[:], ALU.mult, r=[rsp, r_coef], w=[rsm])
            V("tensor_scalar_add", spt[:, 48:64], spt[:, 48:64], 1.0, r=[rsp], w=[rsp])
            V("reciprocal", smo[:, 80:96], spt[:, 48:64], r=[rsp], w=[rsm])
            c = 2 * ti + s
            DMA([(SM[c * 128:(c + 1) * 128, 0:96], smo[:])], rsm, r=[rsm], w=[r_chunk[c]])
        for s in range(2):
            c = 2 * ti + s
            ts_ = slice(s * 128, (s + 1) * 128)
            pairs = [(FMS[c, :, 0:1, :], fmb[:, 8:9, ts_]), (FMS[c, :, 2:10, :], fmb[:, 9:17, ts_])]
            if kind == "own":
                pairs += [(FMS[c, :, 1:2, :], fmb[:, 25:26, ts_]), (FMS[c, :, 10:18, :], fmb[:, 26:34, ts_])]
            DMA(pairs, rfm, r=[rfm], w=[r_chunk[c]])
            for gi, (c0, dst) in enumerate(((0, XTOK), (9, KTOK), (17, VTOK))):
                pb, rp = pbank[5 + (gi % 2)]
                pv = pb[:].bitcast(BF16)
                for j in range(8):
                    T("transpose", pv[:, j * 128:(j + 1) * 128], fmb[:, c0 + j, ts_], identb[:], r=[rfm] + RC,
                      w=[rp], sig=(j == 7))
                tk, rtk = tokp.next()
                if gi % 2 == 0:
                    act(tk[:], pv, AF.Copy, r=[rp], w=[rtk])
                else:
                    V("tensor_copy", tk[:], pv, r=[rp], w=[rtk])
                DMA([(dst[c * 128:(c + 1) * 128, :], tk[:])], rtk, r=[rtk], w=[r_chunk[c]])
            pb, rp = pbank[6]
            pv = pb[:].bitcast(BF16)
            T("transpose", pv[:, 0:128], fmb[:, 8, ts_], identb[:], r=[rfm] + RC, w=[rp])
            bt_, rbt = btkp.next()
            V("tensor_copy", bt_[:], pv[:, 0:128], r=[rp], w=[rbt])
            DMA([(BTOK[c * 128:(c + 1) * 128, :], bt_[:])], rbt, r=[rbt], w=[r_chunk[c]])

    end_stage(ses)
    if _STOP == 'A':
        es.close()
        return nc
    ses = begin_stage()
    Mincl = [tri("mi0", True, 0, False), tri("mi1", False, 0, False)]
    Maft = [tri("ma0", False, 1, False), tri("ma1", True, 1, False)]
    NegT = [tri("ng0", True, 0, False, fillv=NEG, onev=0.0, dt=BF16, rep=4),
            tri("ng1", False, 0, False, fillv=NEG, onev=0.0, dt=BF16, rep=4)]
    statp = Pool("statB", [128, 8], F32, 2)
    seqP = [0, 1] + list(range(2, 2 + 2 * NO))
    seqQ = [1, 0] + list(range(NCH - 1, 2 + 2 * NO - 1, -1)) + list(range(2 + 2 * NO - 1, 1, -1))

    def is_own(c):
        return 2 <= c < 2 + 2 * NO

    xtkp = Pool("xtkB", [128, 1024], BF16, 3)
    btkB = Pool("btkB", [128, 128], BF16, 3)
    fm2p = Pool("fm2", [128, 2, 128], BF16, 3)
    smp = Pool("smB", [128, 96], F32, 3)
    exp_ = Pool("exB", [128, 64], F32, 3)
    xdtp = Pool("xdt", [128, 16, 64], BF16, 2)
    xdwp = Pool("xdw", [128, 16, 64], BF16, 2)
    g2p = Pool("g2", [128, 16, 128], F32, 1)
    argp = Pool("arg", [128, 8, 128], F32, 2)
    ltp = Pool("lt", [128, 8, 128], BF16, 2)
    stp = Pool("stt", [128, 8, 128], BF16, 2)
    tmpp = Pool("tmpB", [128, 8, 64], F32, 2)
    yop = Pool("yo", [128, 1024], F32, 2)
    sst = [sb("sst%d" % d, [128, 512]) for d in range(2)]
    sstb = [sb("sstb%d" % d, [128, 512], BF16) for d in range(2)]
    for d in range(2):
        G("memset", sst[d][0][:], 0.0, w=[sst[d][1]])
        G("memset", sstb[d][0][:], 0.0, w=[sstb[d][1]])

    def ssd_chunk(d, c):
        rev = d
        outp = is_own(c)
        xtk, rxt = xtkp.next()
        btk, rbt = btkB.next()
        fm2, rf2 = fm2p.next()
        sm, rsm = smp.next()
        rc_ = r_chunk[c]
        DMA([(xtk[:], XTOK[c * 128:(c + 1) * 128, :])], rxt, r=[rc_], w=[rxt])
        DMA([(btk[:], BTOK[c * 128:(c + 1) * 128, :])], rbt, r=[rc_], w=[rbt])
        if outp:
            DMA([(fm2[:], FMS[c, :, 0:2, :])], rf2, r=[rc_], w=[rf2])
        DMA([(sm[:], SM[c * 128:(c + 1) * 128, 0:96])], rsm, r=[rc_], w=[rsm])
        dt = sm[:, d * 16:(d + 1) * 16]
        dA = sm[:, 32 + d * 16:32 + (d + 1) * 16]
        st_, rst = sst[d]
        stb, rstb = sstb[d]
        pS, rpS = pbank[0]
        T("matmul", pS[:, 0:16], Mincl[rev][:], dA, start=True, stop=True, r=[rsm] + RC, w=[rpS], sig=False)
        T("matmul", pS[:, 16:32], Maft[rev][:], dA, start=True, stop=True, r=[rsm] + RC, w=[rpS], sig=False)
        T("matmul", pS[:, 32:48], onesf[:], dA, start=True, stop=True, r=[rsm] + RC, w=[rpS])
        ex, rex = exp_.next()
        act(ex[:, 0:48], pS[:, 0:48], AF.Exp, r=[rpS], w=[rex])
        V("tensor_copy", ex[:, 48:64], pS[:, 0:16], r=[rpS], w=[rex])
        ex2, rex2 = exp_.next()
        V("tensor_tensor", ex2[:, 0:16], dt, ex[:, 16:32], ALU.mult, r=[rsm, rex], w=[rex2])
        xv = xtk[:].rearrange("p (h q) -> p h q", h=16)
        xdw, rxw = xdwp.next()
        V("tensor_tensor", xdw[:], xv, bc(ex2[:, 0:16], [128, 16, 64], 2), ALU.mult, r=[rxt, rex2], w=[rxw])
        if outp:
            xdt, rxd = xdtp.next()
            V("tensor_tensor", xdt[:], xv, bc(dt, [128, 16, 64], 2), ALU.mult, r=[rxt, rsm], w=[rxd])
            g2, rg2 = g2p.next()
            V("tensor_tensor", g2[:], bc(Mincl[rev][:], [128, 16, 128], 1), bc(dA, [128, 16, 128], 2), ALU.mult,
              r=[rsm] + RC, w=[rg2])
            yo, ryo = yop.next()
            for hh in range(2):
                for a in range(2):
                    pL, rpL = pbank[1 + a]
                    T("matmul", pL[:], onesf[:], g2[:, hh * 8 + a * 4:hh * 8 + a * 4 + 4, :].rearrange("p h i -> p (h i)"), start=True, stop=False,
                      r=[rg2] + RC, w=[rpL], sig=False)
                    T("matmul", pL[:], identb[:], NegT[rev][:].rearrange("p h i -> p (h i)"), start=False, stop=True, r=RC, w=[rpL])
                arg, rar = argp.next()
                for a in range(2):
                    pL, rpL = pbank[1 + a]
                    h0 = hh * 8 + a * 4
                    V("tensor_tensor", arg[:, a * 4:(a + 1) * 4, :], pL[:].rearrange("p (h i) -> p h i", h=4),
                      bc(ex[:, 48 + h0:48 + h0 + 4], [128, 4, 128], 2), ALU.subtract, r=[rpL, rex], w=[rar])
                lt, rlt = ltp.next()
                act(lt[:], arg[:], AF.Exp, r=[rar], w=[rlt])
                gR = slice(hh * 64, (hh + 1) * 64)
                pC, rpC = pbank[3]
                T("matmul", pC[:, 0:128], fm2[gR, 0, :], fm2[gR, 1, :], start=True, stop=True, r=[rf2], w=[rpC])
                stt, rstt = stp.next()
                V("tensor_tensor", stt[:], lt[:], bc(pC[:, 0:128], [128, 8, 128], 1), ALU.mult, r=[rlt, rpC], w=[rstt])
                pI, rpI = pbank[4]
                T("matmul", pI[:], fm2[gR, 1, :], stb[gR, :], start=True, stop=True, r=[rf2, rstb], w=[rpI])
                pA, rpA = pbank[5]
                for h8 in range(8):
                    T("matmul", pA[:, h8 * 64:(h8 + 1) * 64], stt[:, h8, :], xdt[:, hh * 8 + h8, :], start=True,
                      stop=True, r=[rstt, rxd], w=[rpA], sig=(h8 == 7))
                tmp, rtm = tmpp.next()
                V("tensor_tensor", tmp[:], pI[:].rearrange("p (h q) -> p h q", h=8),
                  bc(ex[:, hh * 8:hh * 8 + 8], [128, 8, 64], 2), ALU.mult, r=[rpI, rex], w=[rtm])
                V("tensor_tensor", yo[:, hh * 512:(hh + 1) * 512], pA[:], tmp[:].rearrange("p h q -> p (h q)"), ALU.add,
                  r=[rpA, rtm], w=[ryo])
            row = (c - 2) * 128
            DMA([(YS[d, row:row + 128, :], yo[:])], ryo, r=[ryo], w=[r_ys[d][c - 2]])
        for g in range(2):
            gR = slice(g * 64, (g + 1) * 64)
            pU, rpU = pbank[6 + g]
            T("matmul", pU[:], btk[:], xdw[:, g * 8:(g + 1) * 8, :].rearrange("p h q -> p (h q)"), start=True,
              stop=True, r=[rbt, rxw], w=[rpU])
            sv = st_[gR, :].rearrange("p (h q) -> p h q", h=8)
            V("tensor_tensor", sv, sv, bc(ex[gR, 32 + g * 8:32 + g * 8 + 8], [64, 8, 64], 2), ALU.mult,
              r=[rst, rex], w=[rst])
            V("tensor_tensor", st_[gR, :], st_[gR, :], pU[gR, :], ALU.add, r=[rst, rpU], w=[rst])
            act(stb[gR, :], st_[gR, :], AF.Copy, r=[rst], w=[rstb])

    for i in range(len(seqQ)):
        ssd_chunk(1, seqQ[i])
        if i < len(seqP):
            ssd_chunk(0, seqP[i])

    end_stage(ses)
    if _STOP == 'B':
        es.close()
        return nc
    ses = begin_stage()
    MinclB = [tri("mib0", True, 0, True), tri("mib1", False, 0, True)]
    MaftB = [tri("mab0", False, 1, True), tri("mab1", True, 1, True)]
    NegP = [tri("np0", False, 0, True, fillv=NEG, onev=0.0, dt=BF16, rep=4),
            tri("np1", True, 0, True, fillv=NEG, onev=0.0, dt=BF16, rep=4)]
    StrictP = [tri("sp0", False, 1, True, dt=BF16, rep=4), tri("sp1", True, 1, True, dt=BF16, rep=4)]
    NegTB = [tri("ntb0", True, 0, True, fillv=NEG, onev=0.0, dt=BF16, rep=4),
             tri("ntb1", False, 0, True, fillv=NEG, onev=0.0, dt=BF16, rep=4)]
    StrictTBn = [tri("stb0", True, 1, True, onev=-1.0, dt=BF16, rep=4), tri("stb1", False, 1, True, onev=-1.0, dt=BF16, rep=4)]
    identrep, _ = sb("identrep", [128, 4, 128])
    for a in range(4):
        V("tensor_copy", identrep[:, a, :], identf[:], r=RC, w=RC)
    onesA, _ = sb("onesA", [128, 128])
    onesB, _ = sb("onesB", [128, 128])
    G("memset", onesA[:], 0.0, w=RC)
    G("memset", onesB[:], 0.0, w=RC)
    G("memset", onesA[0:64, :], 1.0, r=RC, w=RC)
    G("memset", onesB[64:128, :], 1.0, r=RC, w=RC)
    fmkp = Pool("fmk", [128, 16, 128], BF16, 2)
    ktkp = Pool("ktk", [128, 1024], BF16, 2)
    vtkp = Pool("vtk", [128, 1024], BF16, 2)
    smC = Pool("smC", [128, 96], F32, 2)
    exC = Pool("exC", [128, 64], F32, 2)
    g2c = Pool("g2c", [128, 4, 128], F32, 2)
    argc = Pool("argc", [128, 4, 128], F32, 2)
    dpp = Pool("dp", [128, 4, 128], BF16, 2)
    dpsp = Pool("dps", [128, 4, 128], BF16, 2)
    qkp = Pool("qkm", [128, 4, 128], BF16, 2)
    dptp = Pool("dpt", [128, 4, 128], BF16, 2)
    wtp = Pool("wtm", [128, 4, 128], F32, 2)
    qktp = Pool("qkt", [128, 4, 128], BF16, 2)
    PP = Pool("Pm", [128, 4, 128], F32, 3)
    PTP = Pool("PTm", [128, 4, 128], F32, 3)
    XP = Pool("Xm", [128, 4, 128], F32, 3)
    xbp = Pool("xb", [128, 4, 128], BF16, 2)
    vbp = Pool("vb", [128, 4, 128], BF16, 2)
    kbgp = Pool("kbg", [128, 4, 128], BF16, 2)
    kdcp = Pool("kdc", [128, 4, 128], BF16, 2)
    egrp = Pool("egr", [128, 4, 128], BF16, 2)
    qdtp = Pool("qdt", [128, 4, 128], BF16, 2)
    nwtp = Pool("nwt", [128, 4, 128], BF16, 2)
    vnbp = Pool("vnb", [128, 4, 128], BF16, 2)
    goutp = Pool("gout", [128, 1024], F32, 2)
    gS = [sb("gS%d" % d, [128, 8, 128]) for d in range(2)]
    gSb = [sb("gSb%d" % d, [128, 8, 128], BF16) for d in range(2)]
    for d in range(2):
        G("memset", gS[d][0][:], 0.0, w=[gS[d][1]])
        G("memset", gSb[d][0][:], 0.0, w=[gSb[d][1]])

    def gdn_chunk(d, c):
        rev = d
        outp = is_own(c)
        rc_ = r_chunk[c]
        fmk, rfk = fmkp.next()
        ktk, rkt = ktkp.next()
        vtk, rvt = vtkp.next()
        sm, rsm = smC.next()
        DMA([(fmk[:, 0:(16 if outp else 8), :], FMS[c, :, 2:(18 if outp else 10), :])], rfk, r=[rc_], w=[rfk])
        DMA([(ktk[:], KTOK[c * 128:(c + 1) * 128, :])], rkt, r=[rc_], w=[rkt])
        DMA([(vtk[:], VTOK[c * 128:(c + 1) * 128, :])], rvt, r=[rc_], w=[rvt])
        DMA([(sm[:], SM[c * 128:(c + 1) * 128, 0:96])], rsm, r=[rc_], w=[rsm])
        g = sm[:, 64 + d * 8:64 + (d + 1) * 8]
        beta = sm[:, 80 + d * 8:80 + (d + 1) * 8]
        S_, rS = gS[d]
        Sb, rSb = gSb[d]
        pS, rpS = pbank[7]
        T("matmul", pS[:, 0:8], MinclB[rev][:], g, start=True, stop=True, r=[rsm] + RC, w=[rpS], sig=False)
        T("matmul", pS[:, 8:16], MaftB[rev][:], g, start=True, stop=True, r=[rsm] + RC, w=[rpS], sig=False)
        T("matmul", pS[:, 16:24], onesA[:], g, start=True, stop=True, r=[rsm] + RC, w=[rpS], sig=False)
        T("matmul", pS[:, 24:32], onesB[:], g, start=True, stop=True, r=[rsm] + RC, w=[rpS])
        ex, rex = exC.next()
        act(ex[:, 0:32], pS[:, 0:32], AF.Exp, r=[rpS], w=[rex])
        V("tensor_copy", ex[:, 32:40], pS[:, 0:8], r=[rpS], w=[rex])
        V("tensor_scalar_mul", ex[:, 40:48], beta, -1.0, r=[rsm], w=[rex])
        V("tensor_tensor", ex[:, 48:56], beta, ex[:, 0:8], ALU.mult, r=[rsm, rex], w=[rex])
        gout, rgo = goutp.next() if outp else (None, None)
        for hg in range(2):
            h0 = hg * 4
            hs = slice(h0, h0 + 4)
            g2, rg2 = g2c.next()
            V("tensor_tensor", g2[:], bc(MinclB[rev][:], [128, 4, 128], 1), bc(g[:, hs], [128, 4, 128], 2), ALU.mult,
              r=[rsm] + RC, w=[rg2])
            pZ, rpZ = pbank[0]
            g2f = g2[:].rearrange("p h i -> p (h i)")
            T("matmul", pZ[:], negonesf[:], g2f, start=True, stop=False, r=[rg2] + RC, w=[rpZ], sig=False)
            T("matmul", pZ[:], identb[:], NegP[rev][:].rearrange("p h i -> p (h i)"), start=False, stop=True, r=RC,
              w=[rpZ])
            arg, rar = argc.next()
            V("tensor_tensor", arg[:], pZ[:].rearrange("p (h i) -> p h i", h=4), bc(ex[:, 32 + h0:32 + h0 + 4], [128, 4, 128], 2),
              ALU.add, r=[rpZ, rex], w=[rar])
            dp, rdp = dpp.next()
            act(dp[:], arg[:], AF.Exp, r=[rar], w=[rdp])
            dps, rds = dpsp.next()
            V("tensor_tensor", dps[:], dp[:], StrictP[rev][:], ALU.mult, r=[rdp] + RC, w=[rds])
            pK, rpK = pbank[1]
            for a in range(4):
                T("matmul", pK[:, a * 128:(a + 1) * 128], fmk[:, h0 + a, :], fmk[:, h0 + a, :], start=True, stop=True,
                  r=[rfk], w=[rpK], sig=(a == 3))
            Am, rA = PP.next()
            for a in range(4):
                V("scalar_tensor_tensor", out=Am[:, a, :], in0=pK[:, a * 128:(a + 1) * 128],
                  scalar=ex[:, 40 + h0 + a:41 + h0 + a], in1=dps[:, a, :], op0=ALU.mult, op1=ALU.mult,
                  r=[rpK, rex, rds], w=[rA])
            pZ2, rpZ2 = pbank[3]
            T("matmul", pZ2[:], onesf[:], g2f, start=True, stop=False, r=[rg2] + RC, w=[rpZ2], sig=False)
            T("matmul", pZ2[:], identb[:], NegTB[rev][:].rearrange("p h i -> p (h i)"), start=False, stop=True, r=RC,
              w=[rpZ2])
            arg2, rar2 = argc.next()
            V("tensor_tensor", arg2[:], pZ2[:].rearrange("p (h i) -> p h i", h=4), bc(ex[:, 32 + h0:32 + h0 + 4], [128, 4, 128], 2),
              ALU.subtract, r=[rpZ2, rex], w=[rar2])
            dpt, rdpt = dptp.next()
            act(dpt[:], arg2[:], AF.Exp, r=[rar2], w=[rdpt])
            bd, rbd = g2c.next()
            V("tensor_tensor", bd[:], bc(identf[:], [128, 4, 128], 1), bc(beta[:, hs], [128, 4, 128], 2), ALU.mult,
              r=[rsm] + RC, w=[rbd])
            pBr, rpBr = pbank[3]
            T("matmul", pBr[:], onesf[:], bd[:].rearrange("p h i -> p (h i)"), start=True, stop=True, r=[rbd] + RC, w=[rpBr])
            wt_, rwt = wtp.next()
            V("tensor_tensor", wt_[:], pBr[:].rearrange("p (h i) -> p h i", h=4), StrictTBn[rev][:], ALU.mult, r=[rpBr] + RC, w=[rwt])
            V("tensor_tensor", wt_[:], wt_[:], dpt[:], ALU.mult, r=[rwt, rdpt], w=[rwt])
            ATm, rAT = PTP.next()
            Xm, rX = XP.next()
            V("tensor_tensor", ATm[:], pK[:].rearrange("p (h i) -> p h i", h=4), wt_[:], ALU.mult, r=[rpK, rwt], w=[rAT])
            V("tensor_tensor", Xm[:], ATm[:], identrep[:], ALU.add, r=[rAT] + RC, w=[rX])
            if outp:
                pQ, rpQ = pbank[2]
                for a in range(4):
                    T("matmul", pQ[:, a * 128:(a + 1) * 128], fmk[:, h0 + a, :], fmk[:, 8 + h0 + a, :], start=True,
                      stop=True, r=[rfk], w=[rpQ], sig=(a == 3))
                qkt, rqt = qktp.next()
                V("tensor_tensor", qkt[:], pQ[:].rearrange("p (h i) -> p h i", h=4), dpt[:], ALU.mult, r=[rpQ, rdpt], w=[rqt])
                pG, rpG = pbank[0]
                T("matmul", pG[:], onesf[:], g2f, start=True, stop=True, r=[rg2] + RC, w=[rpG])
                egr, reg = egrp.next()
                act(egr[:], pG[:].rearrange("p (h i) -> p h i", h=4), AF.Exp, r=[rpG], w=[reg])
                qdt, rqd = qdtp.next()
                V("tensor_tensor", qdt[:], fmk[:, 8 + h0:8 + h0 + 4, :], egr[:], ALU.mult, r=[rfk, reg], w=[rqd])
            if _KC <= 2:
                continue
            Pm, rP, PTm, rPT = Am, rA, ATm, rAT
            for s in range(6):
                last = (s == 5)
                if not last:
                    pP_, rpP = pbank[1]
                    for a in range(4):
                        T("matmul", pP_[:, a * 128:(a + 1) * 128], PTm[:, a, :], Pm[:, a, :], start=True, stop=True,
                          r=[rP, rPT], w=[rpP], sig=(a == 3))
                    if s < 4:
                        pPT_, rpPT = pbank[2]
                        for a in range(4):
                            T("matmul", pPT_[:, a * 128:(a + 1) * 128], Pm[:, a, :], PTm[:, a, :], start=True,
                              stop=True, r=[rP, rPT], w=[rpPT], sig=(a == 3))
                if s >= 1:
                    pX, rpX = pbank[0]
                    for a in range(4):
                        T("matmul", pX[:, a * 128:(a + 1) * 128], Pm[:, a, :], Xm[:, a, :], start=True, stop=True,
                          r=[rX, rP], w=[rpX], sig=(a == 3))
                if s >= 1:
                    if last:
                        xb, rxb = xbp.next()
                        V("tensor_tensor", xb[:], pX[:].rearrange("p (h i) -> p h i", h=4), Xm[:], ALU.add, r=[rpX, rX], w=[rxb])
                    else:
                        Xn, rXn = XP.next()
                        V("tensor_tensor", Xn[:], pX[:].rearrange("p (h i) -> p h i", h=4), Xm[:], ALU.add, r=[rpX, rX], w=[rXn])
                        Xm, rX = Xn, rXn
                if not last:
                    Pn, rPn = PP.next()
                    act(Pn[:], pP_[:].rearrange("p (h i) -> p h i", h=4), AF.Copy, r=[rpP], w=[rPn])
                    if s < 4:
                        PTn, rPTn = PTP.next()
                        act(PTn[:], pPT_[:].rearrange("p (h i) -> p h i", h=4), AF.Copy, r=[rpPT], w=[rPTn])
                        PTm, rPT = PTn, rPTn
                    Pm, rP = Pn, rPn
            if _KC <= 3:
                continue
            kv = ktk[:].rearrange("p (h q) -> p h q", h=8)[:, hs, :]
            vv = vtk[:].rearrange("p (h q) -> p h q", h=8)[:, hs, :]
            vb, rvb = vbp.next()
            kbg, rkb = kbgp.next()
            kdc, rkd = kdcp.next()
            V("tensor_tensor", vb[:], vv, bc(beta[:, hs], [128, 4, 128], 2), ALU.mult, r=[rvt, rsm], w=[rvb])
            V("tensor_tensor", kbg[:], kv, bc(ex[:, 48 + h0:48 + h0 + 4], [128, 4, 128], 2), ALU.mult, r=[rkt, rex], w=[rkb])
            V("tensor_tensor", kdc[:], kv, bc(ex[:, 8 + h0:8 + h0 + 4], [128, 4, 128], 2), ALU.mult, r=[rkt, rex], w=[rkd])
            pW, rpW = pbank[1]
            for a in range(4):
                T("matmul", pW[:, a * 128:(a + 1) * 128], kbg[:, a, :], xb[:, a, :], start=True, stop=True,
                  r=[rkb, rxb], w=[rpW], sig=(a == 3))
            nwt, rnw = nwtp.next()
            A("mul", nwt[:], pW[:].rearrange("p (h i) -> p h i", h=4), -1.0, r=[rpW], w=[rnw])
            if _KC <= 4:
                continue
            order = (0, 1) if rev == 0 else (1, 0)
            for sub in order:
                Rr = slice(sub * 64, (sub + 1) * 64)
                pV, rpV = pbank[4]
                for a in range(4):
                    T("matmul", pV[:, a * 128:(a + 1) * 128], xb[:, a, :], vb[:, a, :], start=True, stop=False,
                      r=[rxb, rvb], w=[rpV], sig=False)
                    T("matmul", pV[:, a * 128:(a + 1) * 128], nwt[:, a, :], Sb[:, h0 + a, :], start=False, stop=True,
                      r=[rnw, rSb], w=[rpV], sig=(a == 3))
                vnb, rvn = vnbp.next()
                act(vnb[Rr, :, :], pV[Rr, :].rearrange("p (h i) -> p h i", h=4), AF.Copy, r=[rpV], w=[rvn])
                if outp:
                    pO, rpO = pbank[6]
                    for a in range(4):
                        T("matmul", pO[:, a * 128:(a + 1) * 128], qdt[:, a, :], Sb[:, h0 + a, :], start=True,
                          stop=False, r=[rqd, rSb], w=[rpO], sig=False)
                        T("matmul", pO[:, a * 128:(a + 1) * 128], qkt[Rr, a, :], vnb[Rr, a, :], start=False,
                          stop=True, r=[rqt, rvn], w=[rpO], sig=(a == 3))
                    V("tensor_copy", gout[Rr, h0 * 128:(h0 + 4) * 128], pO[Rr, :], r=[rpO], w=[rgo])
                pU, rpU = pbank[5]
                for a in range(4):
                    T("matmul", pU[:, a * 128:(a + 1) * 128], kdc[Rr, a, :], vnb[Rr, a, :], start=True, stop=True,
                      r=[rkd, rvn], w=[rpU], sig=(a == 3))
                edc = ex[:, 16 + sub * 8 + h0:16 + sub * 8 + h0 + 4]
                V("tensor_tensor", S_[:, hs, :], S_[:, hs, :], bc(edc, [128, 4, 128], 2), ALU.mult, r=[rS, rex], w=[rS])
                V("tensor_tensor", S_[:, hs, :], S_[:, hs, :], pU[:].rearrange("p (h i) -> p h i", h=4), ALU.add,
                  r=[rS, rpU], w=[rS])
                act(Sb[:, hs, :], S_[:, hs, :], AF.Copy, r=[rS], w=[rSb])
        if outp:
            row = (c - 2) * 128
            DMA([(YG[d, row:row + 128, :], gout[:])], rgo, r=[rgo], w=[r_yg[d][c - 2]])

    for i in range(len(seqQ)):
        gdn_chunk(1, seqQ[i])
        if i < len(seqP):
            gdn_chunk(0, seqP[i])

    end_stage(ses)
    if _STOP == 'C':
        es.close()
        return nc
    ses = begin_stage()
    WO, r_wo = sb("WO", [128, 16, 1024], BF16)
    DMA([(WO[:], WOB.rearrange("(k p) n -> p k n", p=128))], r_wo, r=[r_wob], w=[r_wo])
    cstD, r_cstD = sb("cstD", [128, 8, 1024])
    DMA([(cstD[:], CST[:, 4:12, :])], r_cstD, r=[r_cst], w=[r_cstD])
    A2, B2, G1, G2m = cstD[:, 0, :], cstD[:, 1, :], cstD[:, 2, :], cstD[:, 3, :]
    gfin, gssd, ggdn, Dbc = cstD[:, 4, :], cstD[:, 5, :], cstD[:, 6, :], cstD[:, 7, :]
    r_gn = r_cstD
    r_modl = r_cstD
    r_A2 = r_cstD
    ldp = Pool("ldD", [128, 4, 1024], F32, 1)
    xtD = Pool("xtD", [128, 1024], BF16, 1)
    zgD = Pool("zgD", [128, 2048], BF16, 1)
    xrD = Pool("xrD", [128, 1024], F32, 1)
    t1p = Pool("t1D", [128, 1024], F32, 2)
    t2p = Pool("t2D", [128, 1024], F32, 2)
    ylp = Pool("yl", [128, 2048], BF16, 1)
    ylTp = Pool("ylT", [128, 16, 128], BF16, 1)
    x1p = Pool("x1", [128, TS, 1024], F32, 1)
    h2p = Pool("h2", [128, 1024], BF16, 1)
    h2Tp = Pool("h2T", [128, 8, TS * 128], BF16, 1)
    hidp = Pool("hid", [128, 32, TS * 128], BF16, 1)
    relp = Pool("rel", [128, 512], BF16, 2)
    w1pp = Pool("w1p", [128, 8, 512], BF16, 2)
    w2pp = Pool("w2p", [128, 32, 128], BF16, 2)
    outp_ = Pool("outD", [128, 1024], F32, 1)
    stD = Pool("stD", [128, 32], F32, 3)
    final_tokens = []
    W1B_v2 = W1B.rearrange("(k p) n -> p k n", p=128)
    W2B_v2 = W2B.rearrange("(f p) n -> p f n", p=128)
    nsup = (2 * NO) // TS
    for su in range(nsup):
        x1, rx1 = x1p.next()
        h2T, rh2T = h2Tp.next()
        for tsub in range(TS):
            tt = su * TS + tsub
            c = 2 + tt
            row = tt * 128
            ld, rld = ldp.next()
            DMA([(ld[:, 0, :], YS[0, row:row + 128, :])], rld, r=[r_ys[0][tt]], w=[rld])
            DMA([(ld[:, 1, :], YS[1, row:row + 128, :])], rld, r=[r_ys[1][tt]], w=[rld])
            DMA([(ld[:, 2, :], YG[0, row:row + 128, :])], rld, r=[r_yg[0][tt]], w=[rld])
            DMA([(ld[:, 3, :], YG[1, row:row + 128, :])], rld, r=[r_yg[1][tt]], w=[rld])
            xtk, rxt = xtD.next()
            DMA([(xtk[:], XTOK[c * 128:(c + 1) * 128, :])], rxt, r=[r_chunk[c]], w=[rxt])
            zg, rzg = zgD.next()
            DMA([(zg[:], ZG[row:row + 128, :])], rzg, r=[r_zg[tt]], w=[rzg])
            xr, rxr = xrD.next()
            DMA([(xr[:], xall[256 + row:256 + row + 128, :])], rxr, w=[rxr])
            st, rst = stD.next()
            G("memset", st[:], 0.0, w=[rst])
            t1, rt1 = t1p.next()
            t2, rt2 = t2p.next()
            yl, ryl = ylp.next()
            V("tensor_tensor", t1[:], ld[:, 0, :], ld[:, 1, :], ALU.add, r=[rld], w=[rt1])
            V("tensor_tensor", t2[:], xtk[:], Dbc, ALU.mult, r=[rxt, r_gn], w=[rt2])
            V("tensor_tensor", t1[:], t1[:], t2[:], ALU.add, r=[rt1, rt2], w=[rt1])
            V("tensor_tensor", t1[:], t1[:], zg[:, 0:1024], ALU.mult, r=[rt1, rzg], w=[rt1])
            for g in range(2):
                act(junk[:, 0:512], t1[:, g * 512:(g + 1) * 512], AF.Square, accum_out=st[:, g:g + 1], r=[rt1, rst],
                    w=[r_junk, rst])
            act(st[:, 2:4], st[:, 0:2], AF.Ln, scale=1.0 / 512, bias=EPS, r=[rst], w=[rst])
            act(st[:, 4:6], st[:, 2:4], AF.Exp, scale=-0.5, r=[rst], w=[rst])
            for g in range(2):
                V("scalar_tensor_tensor", out=yl[:, g * 512:(g + 1) * 512], in0=t1[:, g * 512:(g + 1) * 512],
                  scalar=st[:, 4 + g:5 + g], in1=gssd[:, g * 512:(g + 1) * 512], op0=ALU.mult, op1=ALU.mult,
                  r=[rt1, rst, r_gn], w=[ryl])
            t3, rt3 = t1p.next()
            t4, rt4 = t2p.next()
            V("tensor_tensor", t3[:], ld[:, 2, :], ld[:, 3, :], ALU.add, r=[rld], w=[rt3])
            V("tensor_tensor", t4[:], t3[:], t3[:], ALU.mult, r=[rt3], w=[rt4])
            V("tensor_reduce", out=st[:, 8:16], in_=t4[:].rearrange("p (h q) -> p h q", h=8), axis=AX.X, op=ALU.add,
              r=[rt4], w=[rst])
            act(st[:, 16:24], st[:, 8:16], AF.Ln, scale=1.0 / 128, bias=EPS, r=[rst], w=[rst])
            act(st[:, 24:32], st[:, 16:24], AF.Exp, scale=-0.5, r=[rst], w=[rst])
            V("tensor_tensor", t3[:].rearrange("p (h q) -> p h q", h=8), t3[:].rearrange("p (h q) -> p h q", h=8),
              bc(st[:, 24:32], [128, 8, 128], 2), ALU.mult, r=[rt3, rst], w=[rt3])
            V("tensor_tensor", t3[:], t3[:], ggdn, ALU.mult, r=[rt3, r_gn], w=[rt3])
            V("tensor_tensor", yl[:, 1024:2048], t3[:], zg[:, 1024:2048], ALU.mult, r=[rt3, rzg], w=[ryl])
            ylT, rylT = ylTp.next()
            for half in range(2):
                pb, rp = pbank[half]
                pv = pb[:].bitcast(BF16)
                for j in range(8):
                    k = half * 8 + j
                    T("transpose", pv[:, j * 128:(j + 1) * 128], yl[:, k * 128:(k + 1) * 128], identb[:], r=[ryl] + RC,
                      w=[rp], sig=(j == 7))
                if half == 0:
                    act(ylT[:, 0:8, :], pv.rearrange("p (k t) -> p k t", k=8), AF.Copy, r=[rp], w=[rylT])
                else:
                    V("tensor_copy", ylT[:, 8:16, :], pv.rearrange("p (k t) -> p k t", k=8), r=[rp], w=[rylT])
            for nh in range(2):
                pb, rp = pbank[2 + nh]
                for k in range(16):
                    T("matmul", pb[:], ylT[:, k, :], WO[:, k, nh * 512:(nh + 1) * 512], start=(k == 0), stop=(k == 15),
                      r=[rylT, r_wo], w=[rp], sig=(k == 15))
                cs = slice(nh * 512, (nh + 1) * 512)
                V("tensor_tensor", t2[:, cs], pb[:], G1[:, cs], ALU.mult, r=[rp, r_modl], w=[rt2])
                V("tensor_tensor", x1[:, tsub, cs], xr[:, cs], t2[:, cs], ALU.add, r=[rxr, rt2], w=[rx1])
            act(junk[:], x1[:, tsub, :], AF.Square, accum_out=st[:, 6:7], r=[rx1, rst], w=[r_junk, rst])
            act(st[:, 7:8], st[:, 6:7], AF.Ln, scale=1.0 / 1024, bias=EPS, r=[rst], w=[rst])
            act(st[:, 6:7], st[:, 7:8], AF.Exp, scale=-0.5, r=[rst], w=[rst])
            h2, rh2 = h2p.next()
            V("scalar_tensor_tensor", out=t4[:], in0=x1[:, tsub, :], scalar=st[:, 6:7], in1=A2, op0=ALU.mult,
              op1=ALU.mult, r=[rx1, rst, r_A2], w=[rt4])
            V("tensor_tensor", h2[:], t4[:], B2, ALU.add, r=[rt4, r_modl], w=[rh2])
            pb, rp = pbank[4]
            pv = pb[:].bitcast(BF16)
            for k in range(8):
                T("transpose", pv[:, k * 128:(k + 1) * 128], h2[:, k * 128:(k + 1) * 128], identb[:], r=[rh2] + RC,
                  w=[rp], sig=(k == 7))
            act(h2T[:, :, tsub * 128:(tsub + 1) * 128], pv.rearrange("p (k t) -> p k t", k=8), AF.Copy, r=[rp], w=[rh2T])
        NTT = TS * 128
        hid, rhid = hidp.next()
        for pi in range(8):
            w1t, rw1 = w1pp.next()
            DMA([(w1t[:], W1B_v2[:, :, pi * 512:(pi + 1) * 512])], rw1, r=[r_w1b], w=[rw1])
            for fj in range(4):
                f = pi * 4 + fj
                pb, rp = pbank[5 + (f % 2)]
                for k in range(8):
                    T("matmul", pb[:, 0:NTT], w1t[:, k, fj * 128:(fj + 1) * 128], h2T[:, k, :], start=(k == 0),
                      stop=(k == 7), r=[rw1, rh2T], w=[rp], sig=(k == 7))
                rel, rrel = relp.next()
                act(rel[:, 0:NTT], pb[:, 0:NTT], AF.Relu, r=[rp], w=[rrel])
                V("tensor_tensor", hid[:, f, :], rel[:, 0:NTT], rel[:, 0:NTT], ALU.mult, r=[rrel], w=[rhid])
        x2, rx2 = x1, rx1
        for pi in range(8):
            w2t, rw2 = w2pp.next()
            DMA([(w2t[:], W2B_v2[:, :, pi * 128:(pi + 1) * 128])], rw2, r=[r_w2b], w=[rw2])
            cs = slice(pi * 128, (pi + 1) * 128)
            for tsub in range(TS):
                pb, rp = pbank[2 + (tsub % 2)]
                for f in range(32):
                    T("matmul", pb[:, 0:128], hid[:, f, tsub * 128:(tsub + 1) * 128], w2t[:, f, :], start=(f == 0),
                      stop=(f == 31), r=[rhid, rw2], w=[rp], sig=(f == 31))
                t2, rt2 = t2p.next()
                V("tensor_tensor", t2[:, 0:128], pb[:, 0:128], G2m[:, cs], ALU.mult, r=[rp, r_modl], w=[rt2])
                V("tensor_tensor", x2[:, tsub, cs], x2[:, tsub, cs], t2[:, 0:128], ALU.add, r=[rx2, rt2], w=[rx2])
        for tsub in range(TS):
            tt = su * TS + tsub
            row = tt * 128
            st, rst = stD.next()
            G("memset", st[:], 0.0, w=[rst])
            act(junk[:], x2[:, tsub, :], AF.Square, accum_out=st[:, 0:1], r=[rx2, rst], w=[r_junk, rst])
            act(st[:, 1:2], st[:, 0:1], AF.Ln, scale=1.0 / 1024, bias=EPS, r=[rst], w=[rst])
            act(st[:, 2:3], st[:, 1:2], AF.Exp, scale=-0.5, r=[rst], w=[rst])
            ot, rot = outp_.next()
            V("scalar_tensor_tensor", out=ot[:], in0=x2[:, tsub, :], scalar=st[:, 2:3], in1=gfin, op0=ALU.mult,
              op1=ALU.mult, r=[rx2, rst, r_gn], w=[rot])
            final_tokens.append(DMA([(out[row:row + 128, :], ot[:])], rot, r=[rot]))

    end_stage(ses)
    es.close()
    return nc


_CACHE = {}


def _prep_core(b, hf, x, c, ctx, c_ctx, w_mod, b_mod, norm1_g, w_in, ssd_conv_w, ssd_conv_b, ssd_dt_bias,
               ssd_A_log, ssd_D, ssd_norm_g, gdn_conv_w, gdn_dt_bias, gdn_A_log, gdn_norm_g, w_out, norm2_g,
               w_mlp1, w_mlp2, final_g):
    S = x.shape[1]
    H = S // 2
    f32 = np.float32
    if hf == 0:
        own, far, cx = x[b, :H], x[b, H:], ctx[b]
        dP, dQ = 0, 1
    else:
        own, far, cx = x[b, H:][::-1], x[b, :H][::-1], ctx[b][::-1]
        dP, dQ = 1, 0
    xall = np.ascontiguousarray(np.concatenate([cx, own, far], 0), dtype=f32)
    cvec = np.concatenate([c[b].reshape(8, 128).T, c_ctx.reshape(8, 128).T], 1).astype(f32)
    gains = np.stack([norm1_g[0], norm2_g[0], final_g, ssd_norm_g[0], np.tile(gdn_norm_g[0], 8),
                      np.repeat(ssd_D[0], 64)], 0)
    gains = np.ascontiguousarray(np.broadcast_to(gains[None], (128, 6, 1024)), dtype=f32)
    o_z, o_xbc, o_dt, o_qkv, o_gate, o_a, o_b = 0, 1024, 2304, 2336, 5408, 6432, 6448
    xs = np.arange(o_xbc, o_xbc + 1024)
    Bc = np.arange(o_xbc + 1024, o_xbc + 1152)
    Cc = np.arange(o_xbc + 1152, o_xbc + 1280)
    q = np.arange(o_qkv, o_qkv + 1024)
    k = np.arange(o_qkv + 1024, o_qkv + 2048)
    v = np.arange(o_qkv + 2048, o_qkv + 3072)
    dt = lambda d: np.arange(o_dt + d * 16, o_dt + (d + 1) * 16)
    aa = lambda d: np.arange(o_a + d * 8, o_a + (d + 1) * 8)
    bb = lambda d: np.arange(o_b + d * 8, o_b + (d + 1) * 8)
    z = np.arange(o_z, o_z + 1024)
    gate = np.arange(o_gate, o_gate + 1024)
    perm = np.concatenate([xs, Bc, k, v, Cc, q, dt(dP), dt(dQ), aa(dP), aa(dQ), bb(dP), bb(dQ), z, gate])
    assert perm.shape[0] == 6464
    win = np.ascontiguousarray(w_in[0][:, perm], dtype=f32)
    sw, gw = ssd_conv_w[0], gdn_conv_w[0]
    if hf == 1:
        sw, gw = sw[::-1], gw[::-1]
    cw_cols = np.concatenate([sw[:, 0:1024], sw[:, 1024:1152], gw[:, 1024:2048], gw[:, 2048:3072],
                              sw[:, 1152:1280], gw[:, 0:1024]], 1)
    convw = np.ascontiguousarray(cw_cols.reshape(5, 34, 128).transpose(2, 1, 0), dtype=f32)
    sbias = ssd_conv_b[0]
    cb_cols = np.concatenate([sbias[0:1024], sbias[1024:1152], np.zeros(2048, f32), sbias[1152:1280],
                              np.zeros(1024, f32)])
    convb = np.ascontiguousarray(cb_cols.reshape(34, 128).T, dtype=f32)
    sp = np.concatenate([ssd_dt_bias[0, dP], ssd_dt_bias[0, dQ], gdn_dt_bias[0, dP], gdn_dt_bias[0, dQ],
                         ssd_A_log[0, dP], ssd_A_log[0, dQ], gdn_A_log[0, dP], gdn_A_log[0, dQ]])
    smallp = np.ascontiguousarray(np.broadcast_to(sp[None], (128, 96)), dtype=f32)
    return {"xall": xall, "cvec": np.ascontiguousarray(cvec), "wmod": np.ascontiguousarray(w_mod[0], dtype=f32),
            "bmod": np.ascontiguousarray(b_mod[0][None], dtype=f32), "gains": gains, "w_in": win, "convw": convw,
            "convb": convb, "smallp": smallp, "w_out": np.ascontiguousarray(w_out[0], dtype=f32),
            "w1": np.ascontiguousarray(w_mlp1[0], dtype=f32), "w2": np.ascontiguousarray(w_mlp2[0], dtype=f32)}


def kernel(**inp):
    inp = {k: np.asarray(v) for k, v in inp.items()}
    x = inp["x"]
    Bn, S, D = x.shape
    assert Bn == 4 and D == 1024
    H = S // 2
    NO = H // 256
    if NO not in _CACHE:
        _CACHE[NO] = build(NO)
    nc = _CACHE[NO]
    in_maps = []
    for core in range(8):
        in_maps.append(_prep_core(core // 2, core % 2, **inp))
    res = run_bass_kernel_spmd(nc, in_maps, core_ids=list(range(8)))
    outf = np.empty((Bn, S, D), np.float32)
    for core in range(8):
        b, hf = core // 2, core % 2
        o = np.asarray(res.results[core]["out"], dtype=np.float32)
        if hf == 0:
            outf[b, :H] = o
        else:
            outf[b, H:] = o[::-1]
    return outf
```
